# Optimizing a Trainium2 kernel written in Bass

```python
import jax, jax.numpy as jnp
from jax import lax
import numpy as np

D_MODEL = 1024
BATCH = 8
SEQ = 8192
DEPTH = 4
DEC_BATCH = 16
DEC_SEQ = 32
PAST_LEN = 1024

CHUNK = 64
N_PAST_CHUNKS = 8
A_PAST = N_PAST_CHUNKS * CHUNK
BAND = A_PAST + CHUNK
MAX_REL = 256
REL_SIZE = CHUNK + MAX_REL
N_HEADS = 16
HEAD_DIM = D_MODEL // N_HEADS
D_FF = -(-8 * D_MODEL // (3 * 256)) * 256
N_A_LAYERS = (DEPTH + 1) // 2
N_B_LAYERS = DEPTH // 2
Q_BLOCK = 128
RMS_EPS = 1e-6
NEG_INF = -1e30
ATTN_SCALE = HEAD_DIM ** -0.5

kernel_name = 'streaming_hybrid_chunkrel_fox_step'


def rms_norm(x, g):
    xf = x.astype(jnp.float32)
    y = xf * lax.rsqrt(jnp.mean(xf * xf, axis=-1, keepdims=True) + RMS_EPS)
    return (y * g.astype(jnp.float32)).astype(x.dtype)


def attend(q, k, v, bias, mask):
    s = jnp.einsum('bqhd,bkhd->bhqk', q, k).astype(jnp.float32) * ATTN_SCALE + bias
    if mask is not None:
        s = jnp.where(mask, s, NEG_INF)
    p = jax.nn.softmax(s, axis=-1).astype(v.dtype)
    return jnp.einsum('bhqk,bkhd->bqhd', p, v)


def rel_bias_lookup(table, rel):
    idx = jnp.clip(rel, -(CHUNK - 1), MAX_REL) + (CHUNK - 1)
    return table[:, idx].astype(jnp.float32)


def chunk_attn_prompt(q, k, v, table):
    B, S, H, Dh = q.shape
    pad = ((0, 0), (A_PAST, 0), (0, 0), (0, 0))
    kp = jnp.pad(k, pad)
    vp = jnp.pad(v, pad)
    band = jnp.arange(BAND)
    rel = (A_PAST + jnp.arange(CHUNK))[:, None] - band[None, :]
    bias = rel_bias_lookup(table, rel)

    def one_chunk(ci):
        start = ci * CHUNK
        qc = lax.dynamic_slice_in_dim(q, start, CHUNK, axis=1)
        kc = lax.dynamic_slice_in_dim(kp, start, BAND, axis=1)
        vc = lax.dynamic_slice_in_dim(vp, start, BAND, axis=1)
        valid = (start + band) >= A_PAST
        return attend(qc, kc, vc, bias, valid)

    out = lax.map(one_chunk, jnp.arange(S // CHUNK))
    return jnp.moveaxis(out, 0, 1).reshape(B, S, H, Dh)


def chunk_attn_sample(q, k, v, k_cache, v_cache, table):
    L = k_cache.shape[1]
    T = q.shape[1]
    kk = jnp.concatenate([k_cache, k], axis=1)
    vv = jnp.concatenate([v_cache, v], axis=1)
    kpos = jnp.concatenate([jnp.arange(L) - L, jnp.arange(T)])
    rel = jnp.arange(T)[:, None] - kpos[None, :]
    return attend(q, kk, vv, rel_bias_lookup(table, rel), None)


def forget_attn_prompt(q, k, v, logf):
    B, S, H, Dh = q.shape
    F = jnp.cumsum(logf, axis=1).transpose(0, 2, 1)
    kpos = jnp.arange(S)

    def one_block(bi):
        start = bi * Q_BLOCK
        qb = lax.dynamic_slice_in_dim(q, start, Q_BLOCK, axis=1)
        Fq = lax.dynamic_slice_in_dim(F, start, Q_BLOCK, axis=2)
        bias = Fq[:, :, :, None] - F[:, :, None, :]
        mask = kpos[None, :] <= (start + jnp.arange(Q_BLOCK))[:, None]
        return attend(qb, k, v, bias, mask)

    out = lax.map(one_block, jnp.arange(S // Q_BLOCK))
    return jnp.moveaxis(out, 0, 1).reshape(B, S, H, Dh)


def forget_attn_sample(q, k, v, logf, k_cache, v_cache, logf_cache):
    L = k_cache.shape[1]
    T = q.shape[1]
    lf = jnp.concatenate([logf_cache.astype(jnp.float32), logf], axis=1)
    F = jnp.cumsum(lf, axis=1).transpose(0, 2, 1)
    bias = F[:, :, L:, None] - F[:, :, None, :]
    mask = jnp.arange(L + T)[None, :] <= (L + jnp.arange(T))[:, None]
    kk = jnp.concatenate([k_cache, k], axis=1)
    vv = jnp.concatenate([v_cache, v], axis=1)
    return attend(q, kk, vv, bias, mask)


def trunk(x, c, a_mix, b_mix, w_mod, b_mod, g_mix, g_ffn, w_qkv, w_o,
          w_fgate, b_fgate, w_gu, w_down, g_out):
    B, S, _ = x.shape
    a_state, b_state = [], []
    cs = jax.nn.silu(c)
    for l in range(DEPTH):
        mod = (cs @ w_mod[l] + b_mod[l])[:, None, :]
        sh_m, sc_m, ga_m, sh_f, sc_f, ga_f = jnp.split(mod, 6, axis=-1)
        h = rms_norm(x, g_mix[l]) * (1 + sc_m) + sh_m
        qkv = (h @ w_qkv[l]).reshape(B, S, 3, N_HEADS, HEAD_DIM)
        q, k, v = qkv[:, :, 0], qkv[:, :, 1], qkv[:, :, 2]
        if l % 2 == 0:
            o, st = a_mix(l // 2, q, k, v)
            a_state.append(st)
        else:
            ib = l // 2
            logf = jax.nn.log_sigmoid((h @ w_fgate[ib]).astype(jnp.float32)
                                      + b_fgate[ib].astype(jnp.float32))
            o, st = b_mix(ib, q, k, v, logf)
            b_state.append(st)
        x = x + ga_m * (o.reshape(B, S, D_MODEL) @ w_o[l])
        h = rms_norm(x, g_ffn[l]) * (1 + sc_f) + sh_f
        gate, up = jnp.split(h @ w_gu[l], 2, axis=-1)
        x = x + ga_f * ((jax.nn.silu(gate) * up) @ w_down[l])
    return rms_norm(x, g_out), a_state, b_state


def setup_inputs(seed: int = 0) -> dict:
    key = jax.random.key(seed)
    ks = jax.random.split(key, 24)
    D = D_MODEL
    a_len = min(A_PAST, PAST_LEN)
    sd = D ** -0.5
    return {
        'x_prompt': jax.random.normal(ks[0], (BATCH, SEQ, D), jnp.float32),
        'x_sample': jax.random.normal(ks[1], (DEC_BATCH, DEC_SEQ, D), jnp.float32),
        'cache_a_k': jax.random.normal(ks[2], (N_A_LAYERS, DEC_BATCH, a_len, N_HEADS, HEAD_DIM), jnp.float32),
        'cache_a_v': jax.random.normal(ks[3], (N_A_LAYERS, DEC_BATCH, a_len, N_HEADS, HEAD_DIM), jnp.float32),
        'cache_b_k': jax.random.normal(ks[4], (N_B_LAYERS, DEC_BATCH, PAST_LEN, N_HEADS, HEAD_DIM), jnp.float32),
        'cache_b_v': jax.random.normal(ks[5], (N_B_LAYERS, DEC_BATCH, PAST_LEN, N_HEADS, HEAD_DIM), jnp.float32),
        'cache_b_logf': jax.nn.log_sigmoid(
            jax.random.uniform(ks[6], (N_B_LAYERS, DEC_BATCH, PAST_LEN, N_HEADS), jnp.float32, 1.0, 4.0)
            + 0.5 * jax.random.normal(ks[7], (N_B_LAYERS, DEC_BATCH, PAST_LEN, N_HEADS), jnp.float32)),
        'c_prompt': jax.random.normal(ks[8], (BATCH, D), jnp.float32),
        'c_sample': jax.random.normal(ks[9], (DEC_BATCH, D), jnp.float32),
        'w_mod': 0.5 * sd * jax.random.normal(ks[10], (DEPTH, D, 6 * D), jnp.float32),
        'b_mod': 0.02 * jax.random.normal(ks[11], (DEPTH, 6 * D), jnp.float32),
        'g_mix': 1.0 + 0.02 * jax.random.normal(ks[12], (DEPTH, D), jnp.float32),
        'g_ffn': 1.0 + 0.02 * jax.random.normal(ks[13], (DEPTH, D), jnp.float32),
        'w_qkv': sd * jax.random.normal(ks[14], (DEPTH, D, 3 * D), jnp.float32),
        'w_o': sd * jax.random.normal(ks[15], (DEPTH, D, D), jnp.float32),
        'rel_bias': 0.1 * jax.random.normal(ks[16], (N_A_LAYERS, N_HEADS, REL_SIZE), jnp.float32),
        'w_fgate': 0.5 * sd * jax.random.normal(ks[17], (N_B_LAYERS, D, N_HEADS), jnp.float32),
        'b_fgate': jax.random.uniform(ks[18], (N_B_LAYERS, N_HEADS), jnp.float32, 1.0, 4.0),
        'w_gu': sd * jax.random.normal(ks[19], (DEPTH, D, 2 * D_FF), jnp.float32),
        'w_down': (D_FF ** -0.5) * jax.random.normal(ks[20], (DEPTH, D_FF, D), jnp.float32),
        'g_out': 1.0 + 0.02 * jax.random.normal(ks[21], (D,), jnp.float32),
    }


def reference(x_prompt, x_sample, cache_a_k, cache_a_v, cache_b_k, cache_b_v, cache_b_logf,
              c_prompt, c_sample, w_mod, b_mod, g_mix, g_ffn, w_qkv, w_o, rel_bias,
              w_fgate, b_fgate, w_gu, w_down, g_out):
    keep = min(A_PAST, x_prompt.shape[1])

    def a_prompt(i, q, k, v):
        return chunk_attn_prompt(q, k, v, rel_bias[i]), (k[:, -keep:], v[:, -keep:])

    def b_prompt(i, q, k, v, logf):
        return forget_attn_prompt(q, k, v, logf), (k, v, logf)

    def a_sample(i, q, k, v):
        return chunk_attn_sample(q, k, v, cache_a_k[i], cache_a_v[i], rel_bias[i]), (k, v)

    def b_sample(i, q, k, v, logf):
        o = forget_attn_sample(q, k, v, logf, cache_b_k[i], cache_b_v[i], cache_b_logf[i])
        return o, (k, v, logf)

    y_prompt, pa, pb = trunk(x_prompt, c_prompt, a_prompt, b_prompt, w_mod, b_mod, g_mix, g_ffn,
                             w_qkv, w_o, w_fgate, b_fgate, w_gu, w_down, g_out)
    y_sample, sa, sb = trunk(x_sample, c_sample, a_sample, b_sample, w_mod, b_mod, g_mix, g_ffn,
                             w_qkv, w_o, w_fgate, b_fgate, w_gu, w_down, g_out)

    prompt_a_k = jnp.stack([s[0] for s in pa])
    prompt_a_v = jnp.stack([s[1] for s in pa])
    prompt_b_k = jnp.stack([s[0] for s in pb])
    prompt_b_v = jnp.stack([s[1] for s in pb])
    prompt_b_logf = jnp.stack([s[2] for s in pb])
    sample_a_k = jnp.stack([s[0] for s in sa])
    sample_a_v = jnp.stack([s[1] for s in sa])
    sample_b_k = jnp.stack([s[0] for s in sb])
    sample_b_v = jnp.stack([s[1] for s in sb])
    sample_b_logf = jnp.stack([s[2] for s in sb])
    return (y_prompt, y_sample, prompt_a_k, prompt_a_v, prompt_b_k, prompt_b_v, prompt_b_logf,
            sample_a_k, sample_a_v, sample_b_k, sample_b_v, sample_b_logf)
```

```python
import contextlib
import numpy as np
import concourse.bass as bass
import concourse.mybir as mybir
from concourse.bass_utils import run_bass_kernel_spmd

F32 = mybir.dt.float32
BF16 = mybir.dt.bfloat16
AF = mybir.ActivationFunctionType
ALU = mybir.AluOpType

D = 1024
DFF = 2816
NH = 16
HD = 64
SEQ = 8192
PAST = 1024
APAST = 512
DEPTH = 4
NCORES = 8
ENGS = ["pe", "act", "dve", "pool", "sp"]
NEG = -30000.0


class Res:
    __slots__ = ("name", "last_w", "readers", "sem", "semval", "excl")

    def __init__(self, name, excl=False):
        self.name = name
        self.excl = excl
        self.last_w = None
        self.readers = []
        self.sem = None
        self.semval = 0


class Rec:
    def __init__(self, nc):
        self.nc = nc
        self.ins = {e: [] for e in ENGS}
        self.dma_sems = []
        self.pool_dmas = []

    def _deps(self, reads, writes):
        deps = []
        for r in reads:
            if r.last_w is not None:
                deps.append(r.last_w)
            if r.excl:
                deps.extend((t[0], t[1], t[2], "x") if t[0] == "e" else t for t in r.readers)
        for w in writes:
            if w.last_w is not None:
                t = w.last_w
                deps.append((t[0], t[1], t[2], "x") if t[0] == "e" else t)
            deps.extend((t[0], t[1], t[2], "x") if t[0] == "e" else t for t in w.readers)
        return deps

    def op(self, eng, fn, reads=(), writes=(), relax=False):
        deps = self._deps(reads, writes)
        if not relax:
            deps = [d[:3] if d[0] == "e" else d for d in deps]
        idx = len(self.ins[eng])
        self.ins[eng].append([fn, deps, False, None])
        tok = ("e", eng, idx)
        for r in reads:
            r.readers.append(tok)
        for w in writes:
            w.last_w = tok
            w.readers = []
        return tok

    def dma(self, q, fn, semres, reads=(), writes=()):
        deps = self._deps(reads, writes)
        if semres.sem is None:
            semres.sem = {}
            semres.semval = {}
        if q not in semres.sem:
            semres.sem[q] = len(self.dma_sems)
            semres.semval[q] = 0
            self.dma_sems.append((semres, q))
        semres.semval[q] += 16
        tok = ("d", semres.sem[q], semres.semval[q])
        if q == "pool":
            self.pool_dmas.append(tok)
            if len(self.pool_dmas) > 2:
                deps = deps + [self.pool_dmas[-3]]
        self.ins[q].append([fn, deps, False, semres.sem[q]])
        for r in reads:
            r.readers.append(tok)
        for w in writes:
            w.last_w = tok
            w.readers = []
        return tok

    def wait_all(self, eng, toks):
        self.ins[eng].append([None, list(toks), False, None])

    def replay(self):
        nc = self.nc
        for e in ENGS:
            for it in self.ins[e]:
                for d in it[1]:
                    if d[0] == "e":
                        e2, i2 = d[1], d[2]
                        if e2 == e and (e in ("pe", "sp") or len(d) == 4):
                            continue
                        self.ins[e2][i2][2] = True
        cnt = {}
        for e in ENGS:
            c = 0
            for idx, it in enumerate(self.ins[e]):
                if it[2]:
                    c += 1
                cnt[(e, idx)] = c
        with contextlib.ExitStack() as st:
            esem = {e: st.enter_context(nc.semaphore("sem_" + e)) for e in ENGS}
            dsem = [st.enter_context(nc.semaphore("dsem%d" % i)) for i in range(len(self.dma_sems))]
            block = st.enter_context(nc.Block())

            def run(e, engobj):
                waited = {}
                for idx, it in enumerate(self.ins[e]):
                    need = {}
                    for d in it[1]:
                        if d[0] == "e":
                            e2, i2 = d[1], d[2]
                            if e2 == e and (e in ("pe", "sp") or len(d) == 4):
                                continue
                            key = ("e", e2)
                            v = cnt[(e2, i2)]
                            sem = esem[e2]
                        else:
                            _, si, v = d
                            key = ("d", si)
                            sem = dsem[si]
                        if waited.get(key, 0) >= v:
                            continue
                        if key not in need or need[key][1] < v:
                            need[key] = (sem, v)
                    for key, (sem, v) in need.items():
                        engobj.wait_ge(sem, v)
                        waited[key] = v
                    if it[0] is None:
                        continue
                    ins = it[0](engobj)
                    if it[3] is not None:
                        ins.then_inc(dsem[it[3]], 16)
                    elif it[2]:
                        ins.then_inc(esem[e], 1)

            block.tensor(lambda eng: run("pe", eng))
            block.scalar(lambda eng: run("act", eng))
            block.vector(lambda eng: run("dve", eng))
            block.gpsimd(lambda eng: run("pool", eng))
            block.sync(lambda eng: run("sp", eng))


def bc(ap, axis, n):
    a = [list(x) for x in ap.ap]
    assert a[axis][1] == 1, (a, axis)
    a[axis] = [0, n]
    return bass.AP(ap.tensor, ap.offset, a)


def build(NPT=16, NL=DEPTH):
    nc = bass.Bass("TRN2", target_bir_lowering=False)
    R = Rec(nc)

    def din(name, shape, dt=F32):
        return nc.dram_tensor(name, list(shape), dt, kind="ExternalInput").ap()

    def dout(name, shape, dt=F32):
        return nc.dram_tensor(name, list(shape), dt, kind="ExternalOutput").ap()

    def dscr(name, shape, dt):
        return nc.dram_tensor(name, list(shape), dt).ap()

    xp = din("xp", [SEQ, D])
    xs = din("xs", [64, D])
    cak = din("cak", [2, 2, APAST, D])
    cav = din("cav", [2, 2, APAST, D])
    cbk = din("cbk", [2, 2, PAST, D])
    cbv = din("cbv", [2, 2, PAST, D])
    cbl = din("cbl", [2, 2, PAST, NH])
    c3 = din("c3", [3, D])
    w_mod = din("w_mod", [DEPTH, D, 6 * D])
    b_mod = din("b_mod", [DEPTH, 6 * D])
    g_mix = din("g_mix", [DEPTH, D])
    g_ffn = din("g_ffn", [DEPTH, D])
    w_qkv = din("w_qkv", [DEPTH, D, 3 * D])
    w_o = din("w_o", [DEPTH, D, D])
    rel_bias = din("rel_bias", [2, NH, 320])
    w_fg = din("w_fgate", [2, D, NH])
    b_fg = din("b_fgate", [2, NH])
    w_gu = din("w_gu", [DEPTH, D, 2 * DFF])
    w_dn = din("w_down", [DEPTH, DFF, D])
    g_out = din("g_out", [1, D])
    cst = din("cst", [128, 640])

    y_p = dout("y_p", [SEQ, D])
    y_s = dout("y_s", [64, D])
    o_pak = dout("pak", [2, APAST, D])
    o_pav = dout("pav", [2, APAST, D])
    o_pbk = dout("pbk", [2, SEQ, D])
    o_pbv = dout("pbv", [2, SEQ, D])
    o_pbl = dout("pbl", [2, SEQ, NH])
    o_sak = dout("sak", [2, 64, D])
    o_sav = dout("sav", [2, 64, D])
    o_sbk = dout("sbk", [2, 64, D])
    o_sbv = dout("sbv", [2, 64, D])
    o_sbl = dout("sbl", [2, 64, NH])

    wmod_b = dscr("wmod_b", [DEPTH, 12, 128, 8, 512], BF16)
    wqkv_b = dscr("wqkv_b", [DEPTH, 6, 128, 8, 512], BF16)
    wo_b = dscr("wo_b", [DEPTH, 2, 128, 8, 512], BF16)
    wgu_b = dscr("wgu_b", [DEPTH, 11, 128, 8, 512], BF16)
    wdn_b = dscr("wdn_b", [DEPTH, 2, 128, 22, 512], BF16)
    xscr = dscr("xscr", [SEQ + 64, D], F32)
    khist = dscr("khist", [8, 128, SEQ], BF16)
    vhist = dscr("vhist", [NH, 128, SEQ // 128, 128], BF16)
    etab = dscr("etab", [NH, 768], F32)
    ones_d = dscr("ones_d", [1, 1024], BF16)
    r_onesd = Res("ones_d")
    etab_rep = dscr("etab_rep", [NH, 128, 768], F32)

    r_w = {}
    for l in range(DEPTH):
        for nm in ("mod", "qkv", "o", "gu", "dn"):
            r_w[(nm, l)] = Res("w%s%d" % (nm, l))
    r_xscr = [Res("xscr%d" % t) for t in range(NPT + 1)]
    r_khist = Res("khist")
    r_vhist = Res("vhist")
    r_etab = Res("etab")
    r_etabrep = Res("etabrep")
    out_res = []

    st = contextlib.ExitStack()
    r_khs = [Res("khs%d" % i) for i in range(8)]
    r_vhs = [Res("vhs%d" % i) for i in range(8)]

    def sb(name, shape, dt):
        t = st.enter_context(nc.sbuf_tensor(name, list(shape), dt))
        return t, Res(name)

    CST, r_cst = sb("CST", [128, 640], F32)
    IDB, r_idb = sb("IDB", [128, 128], BF16)
    XT, r_xt = sb("XT", [128, 4, D], F32)
    XN = [sb("XN0", [128, D], BF16)] * 2
    SS, r_ss = sb("SS", [128, 8], F32)
    HT, r_ht = sb("HT", [128, 8, 512], BF16)
    QT, r_qt = sb("QT", [128, 8, 512], BF16)
    KT = [sb("KT%d" % i, [128, 8, 512], BF16) for i in range(2)]
    VA = [sb("VA%d" % i, [128, NH, 4, 128], BF16) for i in range(2)]
    STK = [sb("STK0", [128, D], F32)] * 2
    STV = [sb("STV%d" % i, [128, 512], F32) for i in range(2)]
    OT, r_ot = HT, r_ht
    NPT_BUF = 3
    PT = [sb("PT%d" % i, [128, 512], BF16) for i in range(NPT_BUF)]
    SX = [sb("SX0", [128, 128], F32)] * 2
    RC = [sb("RC%d" % i, [128, 512], F32) for i in range(2)]
    TMP = RC
    NSLOT = 3
    WS = [sb("WS%d" % i, [128, 8, 512], BF16) for i in range(NSLOT)]
    AT, r_at = sb("AT", [128, 22, 512], BF16)
    SQ, r_sq = AT[:, 20:22, :].rearrange("p a n -> p (a n)"), r_at
    CT, r_ct = sb("CT", [128, 8, 3], F32)
    CSB, r_csb = sb("CSB", [128, 8, 3], BF16)
    CSBC, r_csbc = AT[:, 0:6, :].rearrange("p a (b n) -> p (a b) n", n=128).rearrange("p (kc b) n -> p kc b n", b=3), r_at
    MODT, r_modt = sb("MODT", [128, 4, 8, 3], F32)
    BMT, r_bmt = sb("BMT", [128, 48], F32)
    GMT, r_gmt = sb("GMT", [128, 2, 8], F32)
    GS, r_gs = sb("GS", [128, 2, 8, 3], F32)
    SH, r_sh = sb("SH", [128, 2, 8, 3], F32)
    GA, r_ga = sb("GA", [128, 2, 2, D], F32)
    GOUT, r_gout = STK[0]
    LF, r_lf = sb("LF", [128, 4, NH], F32)
    AQT128, r_aqt = sb("AQT", [128, 512], BF16)
    AQT = AQT128[0:NH, :]
    PT.append((AQT128, r_aqt))
    ZT, r_zt = sb("ZT", [128, 4, NH], F32)
    BFG, r_bfg = sb("BFG", [128, NH], F32)
    WFG, r_wfg = sb("WFG", [128, 8, NH], BF16)
    FH, r_fh = sb("FH", [128, 64, NH], F32)
    CARU, r_caru = sb("CARU", [128, 5, NH], F32)
    BI, r_bi = sb("BI", [128, 4, 64], F32)
    KH, r_kh = sb("KH", [128, SEQ], BF16)
    VH, r_vh = AT[:, 0:16, :].rearrange("p a n -> p (a n)").rearrange("p (j e) -> p j e", e=128), r_at
    KC, r_kc = KH[:, :].rearrange("p (c k) -> p c k", k=PAST), r_kh
    MT, r_mt = sb("MT", [128, NH, 5, 128], BF16)
    ETS, r_ets = XT[0:NH, 0, 0:768], r_xt
    CKS = XN
    LFC, r_lfc = sb("LFC", [128, 8, NH], F32)

    PS = []
    for i in range(8):
        t = st.enter_context(nc.psum_tensor("ps%d" % i, [128, 512], F32))
        PS.append((t, Res("ps%d" % i, excl=True)))
    rot = {"A": 0, "B": 0}

    def bankA():
        rot["A"] = (rot["A"] + 1) % 4
        return PS[rot["A"]]

    def bankB():
        rot["B"] = (rot["B"] + 1) % 4
        return PS[4 + rot["B"]]

    R.dma("sp", lambda e: e.dma_start(out=CST[:], in_=cst), r_cst, writes=[r_cst])
    UTRI = CST[:, 128:256]
    ONESF = CST[:, 256:384]
    MNEGD = CST[:, 384:512]
    EPSC = CST[:, 512:513]
    R.op("dve", lambda e: e.tensor_copy(out=IDB[:], in_=CST[:, 0:128]), reads=[r_cst], writes=[r_idb])
    for i in range(2):
        R.op("pool", lambda e, i=i: e.memset(VA[i][0][:, :, :, 64:128], 1.0), writes=[VA[i][1]])
    TRS, r_trs = sb("TRS", [64, 128], F32)

    def load_T(pieces, dsts):
        for ap, r0, n in pieces:
            R.dma("sp", lambda e, ap=ap, r0=r0, n=n: e.dma_start(out=TRS[r0:r0 + n, :], in_=ap), r_trs, writes=[r_trs])
        nrow = max(r0 + n for _, r0, n in pieces)
        pt, rp = bankA()
        R.op("pe", lambda e: e.transpose(pt[:, 0:nrow], TRS[0:nrow, :], CST[0:nrow, 0:nrow]), reads=[r_trs, r_cst], writes=[rp])
        for dst, rdst, c0, c1 in dsts:
            R.op("dve", lambda e, dst=dst, c0=c0, c1=c1: e.tensor_copy(out=dst, in_=pt[:, c0:c1]), reads=[rp], writes=[rdst])

    load_T([(c3.rearrange("b (kc p) -> (b kc) p", p=128), 0, 24)],
           [(CT[:, :, :].rearrange("p kc b -> p b kc"), r_ct, 0, 24)])
    R.op("act", lambda e: e.activation(out=CSB[:], in_=CT[:], func=AF.Silu), reads=[r_ct], writes=[r_csb])
    R.op("pool", lambda e: e.memset(KH[:, :], 0.0), writes=[r_kh] + r_khs)
    R.op("pool", lambda e: e.memset(PT[0][0][0:1, :], 1.0), writes=[PT[0][1]])
    for hh in range(2):
        R.dma("sp", lambda e, hh=hh: e.dma_start(out=ones_d[0:1, hh * 512:(hh + 1) * 512], in_=PT[0][0][0:1, :]),
              PT[0][1], reads=[PT[0][1]], writes=[r_onesd])
    QO, r_qo = KT[1]

    def cast_w(nm, l, src, dst, K, N):
        res = r_w[(nm, l)]
        for kc in range(K // 128):
            s_ap = src[l, kc * 128:(kc + 1) * 128, :].rearrange("p (j n) -> p j n", n=512)
            d_ap = dst[l, :, :, kc, :].rearrange("j p n -> p j n")
            R.dma("pool", lambda e, s_ap=s_ap, d_ap=d_ap: e.dma_start(out=d_ap, in_=s_ap), res, writes=[res])

    for l in range(NL):
        cast_w("mod", l, w_mod, wmod_b, D, 6 * D)
        cast_w("qkv", l, w_qkv, wqkv_b, D, 3 * D)
        cast_w("o", l, w_o, wo_b, D, D)
        cast_w("gu", l, w_gu, wgu_b, D, 2 * DFF)
        cast_w("dn", l, w_dn, wdn_b, DFF, D)

    wstate = {"n": 0}
    ring = {"k": 0, "v": 0, "ke": 0, "ko": 0}

    def wload(nm, l, wt, r0, nkc, c0, ncol=512):
        slot, rs = WS[wstate["n"] % NSLOT]
        wstate["n"] += 1
        src = wt[l, c0 // 512, :, r0 // 128:r0 // 128 + nkc, :]
        R.dma("sp", lambda e: e.dma_start(out=slot[:, 0:nkc, 0:ncol], in_=src), rs, reads=[r_w[(nm, l)]], writes=[rs])
        return slot, rs

    import os as _os
    _lim = int(_os.environ.get("DBG_STOP", "1000000"))
    _cnt = [0]

    class _Stop(Exception):
        pass

    def ck(name):
        _cnt[0] += 1
        if _cnt[0] >= _lim:
            print("DBG_STOP at", _cnt[0], name, flush=True)
            raise _Stop()

    tiles = []
    for t in range(NPT):
        tiles.append(dict(kind="p", idx=t, T=512, nsub=4, rows=128, row0=t * 512))
    tiles.append(dict(kind="s", idx=NPT, T=64, nsub=2, rows=32, row0=SEQ))

    def store_out(dst_ap, src_ap, rsrc):
        ro = Res("o")
        out_res.append(ro)
        R.dma("sp", lambda e: e.dma_start(out=dst_ap, in_=src_ap), rsrc, reads=[rsrc], writes=[ro])

    def mod_ga(l, pairs):
        for w, part in enumerate((2, 5)):
            for (b, sl) in pairs:
                R.dma("sp", lambda e, w=w, part=part, sl=sl: e.dma_start(
                    out=GA[:, w, sl, :], in_=bass.AP(b_mod.tensor, l * 6 * D + part * D, [[0, 128], [1, D]])),
                    r_ga, writes=[r_ga])
            for half in range(2):
                slot, rs = wload("mod", l, wmod_b, 0, 8, part * 1024 + half * 512)
                for (b, sl) in pairs:
                    pt, rp = bankB()
                    for kc in range(8):
                        R.op("pe", lambda e, kc=kc, pt=pt, b=b, slot=slot: e.matmul(
                            pt[:, :], lhsT=CSBC[:, kc, b, :], rhs=slot[:, kc, :], start=(kc == 0), stop=(kc == 7)),
                            reads=[rs, r_csbc], writes=[rp])
                    R.op("dve", lambda e, pt=pt, sl=sl, w=w, half=half: e.tensor_tensor(
                        out=GA[:, w, sl, half * 512:(half + 1) * 512], in0=pt[:, :],
                        in1=GA[:, w, sl, half * 512:(half + 1) * 512], op=ALU.add),
                        reads=[rp, r_ga], writes=[r_ga])

    def csbc_fill():
        for kc in range(8):
            for b in range(3):
                R.op("dve", lambda e, kc=kc, b=b: e.tensor_copy(out=CSBC[:, kc, b, :], in_=bc(CSB[:, kc, b:b + 1], 1, 128)),
                     reads=[r_csb], writes=[r_csbc])

    def modulation(l):
        load_T([(b_mod[l, :].rearrange("(o p) -> o p", p=128), 0, 48),
                (g_mix[l, :].rearrange("(o p) -> o p", p=128), 48, 8),
                (g_ffn[l, :].rearrange("(o p) -> o p", p=128), 56, 8)],
               [(BMT[:, :], r_bmt, 0, 48), (GMT[:, :, :].rearrange("p w o -> p (w o)"), r_gmt, 48, 64)])
        for cb in range(12):
            part = cb // 2
            half = cb % 2
            if part in (2, 5):
                continue
            slot, rs = wload("mod", l, wmod_b, 0, 8, cb * 512)
            pi = {0: 0, 1: 1, 3: 2, 4: 3}[part]
            for q4 in range(4):
                oc = half * 4 + q4
                pt, rp = bankA()
                for kc in range(8):
                    R.op("pe", lambda e, kc=kc, pt=pt, q4=q4, slot=slot: e.matmul(
                        pt[:, 0:3], lhsT=slot[:, kc, q4 * 128:(q4 + 1) * 128], rhs=CSB[:, kc, :],
                        start=(kc == 0), stop=(kc == 7)), reads=[rs, r_csb], writes=[rp])
                col = part * 8 + oc
                R.op("dve", lambda e, pt=pt, pi=pi, oc=oc, col=col: e.tensor_scalar(
                    out=MODT[:, pi, oc, :], in0=pt[:, 0:3], scalar1=BMT[:, col:col + 1], scalar2=None, op0=ALU.add),
                    reads=[rp, r_bmt], writes=[r_modt])
        for w in range(2):
            for b in range(3):
                R.op("dve", lambda e, w=w, b=b: e.scalar_tensor_tensor(
                    out=GS[:, w, :, b], in0=MODT[:, 2 * w + 1, :, b], scalar=1.0, in1=GMT[:, w, :],
                    op0=ALU.add, op1=ALU.mult), reads=[r_modt, r_gmt], writes=[r_gs])
                R.op("dve", lambda e, w=w, b=b: e.tensor_copy(out=SH[:, w, :, b], in_=MODT[:, 2 * w, :, b]),
                     reads=[r_modt], writes=[r_sh])
        csbc_fill()
        mod_ga(l, [(0, 0)])

    def rstd_calc(tl):
        nsub, rows = tl["nsub"], tl["rows"]
        for u in range(nsub):
            R.op("act", lambda e, u=u: e.activation(out=SQ[0:rows, :], in_=XT[0:rows, u, :], func=AF.Square,
                                                    accum_out=SS[0:rows, u:u + 1]),
                 reads=[r_xt], writes=[r_sq, r_ss])
        R.op("act", lambda e: e.activation(out=SS[0:rows, 4:4 + nsub], in_=SS[0:rows, 0:nsub], func=AF.Sqrt,
                                           scale=1.0 / D, bias=EPSC[0:rows, :]), reads=[r_ss, r_cst], writes=[r_ss])
        R.op("dve", lambda e: e.reciprocal(out=SS[0:rows, 4:4 + nsub], in_=SS[0:rows, 4:4 + nsub]), reads=[r_ss], writes=[r_ss])

    def norm_to_hT(tl, w):
        nsub, rows = tl["nsub"], tl["rows"]
        rstd_calc(tl)
        for u in range(nsub):
            xn, rxn = XN[u % 2]
            R.op("act", lambda e, u=u, xn=xn: e.activation(out=xn[0:rows, :], in_=XT[0:rows, u, :], func=AF.Copy,
                                                           scale=SS[0:rows, 4 + u:5 + u]), reads=[r_xt, r_ss], writes=[rxn])
            for half in range(2):
                pt, rp = bankA()
                ptb = pt[:].bitcast(BF16)
                for c4 in range(4):
                    c = half * 4 + c4
                    R.op("pe", lambda e, c=c, c4=c4, ptb=ptb, xn=xn: e.transpose(
                        ptb[:, c4 * 128:c4 * 128 + rows], xn[0:rows, c * 128:(c + 1) * 128], IDB[0:rows, 0:rows]),
                        reads=[rxn, r_idb], writes=[rp])
                b = 0 if tl["kind"] == "p" else 1 + u
                for c4 in range(4):
                    c = half * 4 + c4
                    R.op("dve", lambda e, c=c, c4=c4, ptb=ptb, u=u, b=b: e.tensor_scalar(
                        out=HT[:, c, u * rows:(u + 1) * rows], in0=ptb[:, c4 * 128:c4 * 128 + rows],
                        scalar1=GS[:, w, c, b:b + 1], scalar2=SH[:, w, c, b:b + 1], op0=ALU.mult, op1=ALU.add),
                        reads=[rp, r_gs, r_sh], writes=[r_ht])

    def qkv(l, tl, par):
        T, nsub, rows = tl["T"], tl["nsub"], tl["rows"]
        isB = (l % 2 == 1)
        li = l // 2
        kt, rkt = KT[0 if isB else par]
        va, rva = VA[par]
        splitq = isB and tl["kind"] == "p"
        last_p = (tl["kind"] == "p" and tl["idx"] == NPT - 1)
        need_ktok = isB or last_p or tl["kind"] == "s"
        need_vout = need_ktok

        def kv_dst(which, u):
            if tl["kind"] == "s":
                o = {("k", 0): o_sak, ("v", 0): o_sav, ("k", 1): o_sbk, ("v", 1): o_sbv}[(which, l % 2)]
                return o[li, u * 32:(u + 1) * 32, :]
            if isB:
                o = o_pbk if which == "k" else o_pbv
                return o[li, tl["row0"] + u * 128: tl["row0"] + (u + 1) * 128, :]
            o = o_pak if which == "k" else o_pav
            return o[li, u * 128:(u + 1) * 128, :]

        def feat_major(blk, slot, rs):
            for q4 in range(4):
                oc = (blk % 2) * 4 + q4
                pt, rp = bankA()
                for kc in range(8):
                    R.op("pe", lambda e, kc=kc, pt=pt, q4=q4, slot=slot: e.matmul(
                        pt[:, 0:T], lhsT=slot[:, kc, q4 * 128:(q4 + 1) * 128], rhs=HT[:, kc, 0:T],
                        start=(kc == 0), stop=(kc == 7)), reads=[rs, r_ht], writes=[rp])
                if blk < 2 and splitq:
                    R.op("act", lambda e, pt=pt, oc=oc: e.activation(out=QT[0:64, oc, 0:T], in_=pt[0:64, 0:T], func=AF.Copy, scale=0.125),
                         reads=[rp], writes=[r_qt])
                    R.op("act", lambda e, pt=pt, oc=oc: e.activation(out=QO[64:128, oc, 0:T], in_=pt[64:128, 0:T], func=AF.Copy, scale=0.125),
                         reads=[rp], writes=[r_qo])
                elif blk < 2:
                    R.op("act", lambda e, pt=pt, oc=oc: e.activation(out=QT[:, oc, 0:T], in_=pt[:, 0:T], func=AF.Copy, scale=0.125),
                         reads=[rp], writes=[r_qt])
                else:
                    R.op("dve", lambda e, pt=pt, oc=oc: e.tensor_copy(out=kt[:, oc, 0:T], in_=pt[:, 0:T]), reads=[rp], writes=[rkt])

        for blk in (0, 1):
            slot, rs = wload("qkv", l, wqkv_b, 0, 8, blk * 512)
            feat_major(blk, slot, rs)
        kslots = [wload("qkv", l, wqkv_b, 0, 8, blk * 512) for blk in (2, 3)]
        for i, blk in enumerate((2, 3)):
            feat_major(blk, kslots[i][0], kslots[i][1])
        ck("qkv-qk")
        if need_ktok:
            for u in range(nsub):
                stg, rstg = STK[u % 2]
                for half in range(2):
                    slot, rs = kslots[half]
                    pt, rp = bankB()
                    for kc in range(8):
                        R.op("pe", lambda e, kc=kc, pt=pt, u=u, slot=slot: e.matmul(
                            pt[0:rows, :], lhsT=HT[:, kc, u * rows:(u + 1) * rows], rhs=slot[:, kc, :],
                            start=(kc == 0), stop=(kc == 7)), reads=[rs, r_ht], writes=[rp])
                    R.op("dve", lambda e, pt=pt, stg=stg, half=half: e.tensor_copy(out=stg[0:rows, half * 512:(half + 1) * 512], in_=pt[0:rows, :]),
                         reads=[rp], writes=[rstg])
                ck("ktok-mm u%d" % u)
                store_out(kv_dst("k", u), stg[0:rows, :], rstg)
                ck("ktok-store u%d" % u)
        for blk in (4, 5):
            slot, rs = wload("qkv", l, wqkv_b, 0, 8, blk * 512)
            half = blk - 4
            for u in range(nsub):
                pt, rp = bankB()
                for kc in range(8):
                    R.op("pe", lambda e, kc=kc, pt=pt, u=u, slot=slot: e.matmul(
                        pt[0:rows, :], lhsT=HT[:, kc, u * rows:(u + 1) * rows], rhs=slot[:, kc, :],
                        start=(kc == 0), stop=(kc == 7)), reads=[rs, r_ht], writes=[rp])
                R.op("act", lambda e, pt=pt, u=u, half=half: e.activation(
                    out=va[0:rows, half * 8:(half + 1) * 8, u, 0:64],
                    in_=pt[0:rows, :].rearrange("p (h e) -> p h e", e=64), func=AF.Copy),
                    reads=[rp], writes=[rva])
                if need_vout:
                    stg, rstg = STV[(2 * half + u) % 2]
                    R.op("dve", lambda e, pt=pt, stg=stg: e.tensor_copy(out=stg[0:rows, 0:512], in_=pt[0:rows, :]),
                         reads=[rp], writes=[rstg])
                    store_out(kv_dst("v", u)[:, half * 512:(half + 1) * 512], stg[0:rows, 0:512], rstg)
        return need_ktok, kv_dst

    def k_tokmajor(l, tl):
        pass

    def normalize_head(h, po, rpo, q0, nq):
        rc, rrc = RC[h % 2]
        c = h // 2
        pb = (h % 2) * 64
        R.op("dve", lambda e: e.reciprocal(out=rc[64:128, q0:q0 + nq], in_=po[64:128, q0:q0 + nq]), reads=[rpo], writes=[rrc])
        R.op("dve", lambda e: e.tensor_tensor(out=OT[pb:pb + 64, c, q0:q0 + nq], in0=po[0:64, q0:q0 + nq],
                                              in1=rc[64:128, q0:q0 + nq], op=ALU.mult),
             reads=[rpo, rrc], writes=[r_ot])

    def attend(h, ktiles, qoff, nq, exact_recip=True, qap=None, rq=None, nbuf=3):
        c = h // 2
        pb = (h % 2) * 64
        po, rpo = PS[4 + (h % 3)]
        pend = []
        if qap is None:
            qap, rq = QT[pb:pb + 64, c, :], r_qt

        def emit_pv(ktl, pT, rpT, first):
            nk, c0, c1 = ktl["nk"], ktl["c0"], ktl["c1"]
            R.op("pe", lambda e: e.matmul(
                po[:, c0:c1], lhsT=ktl["v"], rhs=pT[0:nk, c0:c1], start=first, stop=False, skip_group_check=True),
                reads=[ktl["rv"], rpT], writes=[rpo])

        for i, ktl in enumerate(ktiles):
            nk, c0, c1 = ktl["nk"], ktl["c0"], ktl["c1"]
            pt, rp = bankA()
            pT, rpT = PT[i % nbuf]
            R.op("pe", lambda e, ktl=ktl, pt=pt, nk=nk, c0=c0, c1=c1: e.matmul(
                pt[0:nk, c0:c1], lhsT=ktl["k"], rhs=qap[:, qoff + c0:qoff + c1], start=True, stop=True),
                reads=[ktl["rk"], rq], writes=[rp])
            if len(pend) >= nbuf - 1:
                emit_pv(*pend.pop(0))
            for gi, (g0, g1, bias, mneg) in enumerate(ktl["groups"]):
                src = pt[0:nk, g0:g1]
                rsrc = rp
                if mneg is not None:
                    sx, rsx = SX[gi % 2]
                    R.op("dve", lambda e, sx=sx, src=src, mneg=mneg, nk=nk, g0=g0, g1=g1: e.tensor_tensor(
                        out=sx[0:nk, 0:g1 - g0], in0=src, in1=mneg, op=ALU.add), reads=[rp, r_cst], writes=[rsx])
                    src = sx[0:nk, 0:g1 - g0]
                    rsrc = rsx
                if bias is not None:
                    R.op("act", lambda e, src=src, pT=pT, nk=nk, g0=g0, g1=g1, bias=bias: e.activation(
                        out=pT[0:nk, g0:g1], in_=src, func=AF.Exp, bias=bias), reads=[rsrc, r_bi], writes=[rpT], relax=True)
                else:
                    R.op("act", lambda e, src=src, pT=pT, nk=nk, g0=g0, g1=g1: e.activation(
                        out=pT[0:nk, g0:g1], in_=src, func=AF.Exp), reads=[rsrc], writes=[rpT], relax=True)
            if ktl.get("mul") is not None:
                mul = ktl["mul"]
                nu = ktl["nu"]
                qw = (c1 - c0) // nu
                pv = pT[0:nk, c0:c1].rearrange("p (u q) -> p u q", q=qw)
                R.op("dve", lambda e, pv=pv, mul=mul: e.tensor_tensor(out=pv, in0=pv, in1=mul, op=ALU.mult),
                     reads=[rpT, r_mt], writes=[rpT])
            pend.append((ktl, pT, rpT, i == 0))
        while pend:
            emit_pv(*pend.pop(0))
        normalize_head_off(h, po, rpo, qoff, nq, exact_recip)

    def normalize_head_off(h, po, rpo, qoff, nq, exact_recip=True):
        rc, rrc = RC[h % 2]
        c = h // 2
        pb = (h % 2) * 64
        if exact_recip:
            R.op("dve", lambda e: e.reciprocal(out=rc[64:128, 0:nq], in_=po[64:128, 0:nq]), reads=[rpo], writes=[rrc])
        else:
            R.op("act", lambda e: e.activation(out=rc[64:128, 0:nq], in_=po[64:128, 0:nq], func=AF.Ln), reads=[rpo], writes=[rrc])
            R.op("act", lambda e: e.activation(out=rc[64:128, 0:nq], in_=rc[64:128, 0:nq], func=AF.Exp, scale=-1.0), reads=[rrc], writes=[rrc])
        R.op("dve", lambda e: e.tensor_tensor(out=OT[pb:pb + 64, c, qoff:qoff + nq], in0=po[0:64, 0:nq],
                                              in1=rc[64:128, 0:nq], op=ALU.mult),
             reads=[rpo, rrc], writes=[r_ot])

    def build_MT(ia):
        R.dma("sp", lambda e: e.dma_start(out=ETS[:, 64:384], in_=rel_bias[ia, :, :]), r_ets, writes=[r_ets])
        R.op("dve", lambda e: e.tensor_copy(out=ETS[:, 0:64], in_=bc(ETS[:, 64:65], 1, 64)), reads=[r_ets], writes=[r_ets])
        R.op("dve", lambda e: e.tensor_copy(out=ETS[:, 384:768], in_=bc(ETS[:, 383:384], 1, 384)), reads=[r_ets], writes=[r_ets])
        R.op("act", lambda e: e.activation(out=ETS[:, :], in_=ETS[:, :], func=AF.Exp), reads=[r_ets], writes=[r_ets])
        R.dma("sp", lambda e: e.dma_start(out=etab, in_=ETS[:, :]), r_ets, reads=[r_ets], writes=[r_etab])
        R.dma("sp", lambda e: e.dma_start(out=etab_rep, in_=bass.AP(etab.tensor, 0, [[768, NH], [0, 128], [1, 768]])),
              r_etabrep, reads=[r_etab], writes=[r_etabrep])
        for h in range(NH):
            src = bass.AP(etab_rep.tensor, h * 128 * 768 + 127, [[767, 128], [128, 5], [1, 128]])
            R.dma("pool", lambda e, h=h, src=src: e.dma_start(out=MT[:, h, :, :], in_=src), r_mt, reads=[r_etabrep], writes=[r_mt])
        R.op("pool", lambda e: e.memset(MT[64:128, :, 0, 0:64], 0.0), writes=[r_mt])
        R.op("pool", lambda e: e.memset(MT[0:64, :, 4, 64:128], 0.0), writes=[r_mt])

    def attn_A_prompt(tl, par):
        t = tl["idx"]
        for h in range(NH):
            c = h // 2
            pb = (h % 2) * 64
            kts = []
            for g in range(max(0, 4 * t - 4), 4 * t + 4):
                u_lo = max(0, g - 4 * t)
                u_hi = min(3, g - 4 * t + 4)
                src = par if g >= 4 * t else 1 - par
                ku = g % 4
                nu = u_hi - u_lo + 1
                rl = u_lo - (g - 4 * t)
                kts.append(dict(k=KT[src][0][pb:pb + 64, c, ku * 128:(ku + 1) * 128], rk=KT[src][1],
                                v=VA[src][0][:, h, ku, :], rv=VA[src][1], nk=128, c0=u_lo * 128, c1=(u_hi + 1) * 128,
                                groups=[(u_lo * 128, (u_hi + 1) * 128, None, None)],
                                mul=MT[:, h, rl:rl + nu, :], nu=nu))
            attend(h, kts, 0, 512, exact_recip=False, nbuf=4)

    def prep_cache_K(src_cache, li, b, ntile):
        for j in range(ntile):
            cks, rck = CKS[j % 2]
            R.dma("pool", lambda e, cks=cks, j=j: e.dma_start(out=cks[:, :], in_=src_cache[li, b, j * 128:(j + 1) * 128, :]),
                  rck, writes=[rck])
            for half in range(2):
                pt, rp = bankA()
                ptb = pt[:].bitcast(BF16)
                for c4 in range(4):
                    c = half * 4 + c4
                    R.op("pe", lambda e, c=c, c4=c4, ptb=ptb, cks=cks: e.transpose(
                        ptb[:, c4 * 128:(c4 + 1) * 128], cks[:, c * 128:(c + 1) * 128], IDB[:, :]),
                        reads=[rck, r_idb], writes=[rp])
                R.op("dve", lambda e, ptb=ptb, half=half, j=j: e.tensor_copy(
                    out=KC[:, half * 4:(half + 1) * 4, j * 128:(j + 1) * 128],
                    in_=ptb[:, 0:512].rearrange("p (c k) -> p c k", k=128)), reads=[rp], writes=[r_kc])

    def load_cache_V(src_cache, li, b, h, ntile):
        src = src_cache[li, b, 0:ntile * 128, h * 64:(h + 1) * 64].rearrange("(j p) e -> p j e", p=128)
        R.op("pool", lambda e: e.memset(VH[:, 0:ntile, 64:128], 1.0), writes=[r_vh])
        R.dma("pool", lambda e: e.dma_start(out=VH[:, 0:ntile, 0:64], in_=src), r_vh, writes=[r_vh])

    def attn_A_sample(l, tl, par):
        ia = l // 2
        for b in range(2):
            prep_cache_K(cak, ia, b, 4)
            for h in range(NH):
                c = h // 2
                pb = (h % 2) * 64
                load_cache_V(cav, ia, b, h, 4)
                kts = []
                for j in range(4):
                    rl = 4 - j
                    kts.append(dict(k=KC[pb:pb + 64, c, j * 128:(j + 1) * 128], rk=r_kc, v=VH[:, j, :], rv=r_vh, nk=128,
                                    c0=0, c1=32, groups=[(0, 32, None, None)], mul=MT[:, h, rl:rl + 1, 0:32], nu=1))
                kts.append(dict(k=KT[par][0][pb:pb + 64, c, b * 32:(b + 1) * 32], rk=KT[par][1],
                                v=VA[par][0][0:32, h, b, :], rv=VA[par][1], nk=32, c0=0, c1=32,
                                groups=[(0, 32, None, None)], mul=MT[0:32, h, 0:1, 0:32], nu=1))
                attend(h, kts, b * 32, 32)

    def logf_compute(l, tl):
        ib = l // 2
        nsub, rows = tl["nsub"], tl["rows"]
        for u in range(nsub):
            pt, rp = bankB()
            for kc in range(8):
                R.op("pe", lambda e, kc=kc, pt=pt, u=u: e.matmul(
                    pt[0:rows, 0:NH], lhsT=HT[:, kc, u * rows:(u + 1) * rows], rhs=WFG[:, kc, :], start=(kc == 0), stop=(kc == 7)),
                    reads=[r_ht, r_wfg], writes=[rp])
            R.op("dve", lambda e, pt=pt, u=u: e.tensor_tensor(out=ZT[0:rows, u, :], in0=pt[0:rows, 0:NH], in1=BFG[0:rows, :], op=ALU.add),
                 reads=[rp, r_bfg], writes=[r_zt])
        R.op("act", lambda e: e.activation(out=ZT[0:rows, 0:nsub, :], in_=ZT[0:rows, 0:nsub, :], func=AF.Exp, scale=-1.0),
             reads=[r_zt], writes=[r_zt])
        R.op("act", lambda e: e.activation(out=ZT[0:rows, 0:nsub, :], in_=ZT[0:rows, 0:nsub, :], func=AF.Ln, bias=ONESF[0:rows, 0:1]),
             reads=[r_zt, r_cst], writes=[r_zt])
        R.op("dve", lambda e: e.tensor_scalar(out=LF[0:rows, 0:nsub, :], in0=ZT[0:rows, 0:nsub, :], scalar1=-1.0, scalar2=None, op0=ALU.mult),
             reads=[r_zt], writes=[r_lf])
        if tl["kind"] == "p":
            dst = o_pbl[ib, tl["row0"]:tl["row0"] + 512, :].rearrange("(u p) h -> p u h", p=128)
        else:
            dst = o_sbl[ib, :, :].rearrange("(u p) h -> p u h", p=32)
        store_out(dst, LF[0:rows, 0:nsub, :], r_lf)

    def cumsum_tile(src_ap, rsrc, rows, fh_dst, cin, cout):
        pt, rp = bankB()
        R.op("pe", lambda e: e.matmul(pt[0:rows, 0:NH], lhsT=UTRI[0:rows, 0:rows], rhs=src_ap, start=True, stop=True),
             reads=[rsrc, r_cst], writes=[rp])
        R.op("pe", lambda e: e.matmul(pt[:, 32:32 + NH], lhsT=ONESF[0:rows, :], rhs=src_ap, start=True, stop=True, skip_group_check=True),
             reads=[rsrc, r_cst], writes=[rp])
        R.op("dve", lambda e: e.tensor_tensor(out=fh_dst, in0=pt[0:rows, 0:NH], in1=CARU[0:rows, cin, :], op=ALU.add),
             reads=[rp, r_caru], writes=[r_fh])
        R.op("dve", lambda e: e.tensor_tensor(out=CARU[:, cout, :], in0=pt[:, 32:32 + NH], in1=CARU[:, cin, :], op=ALU.add),
             reads=[rp, r_caru], writes=[r_caru])

    def attn_B_prompt(l, tl, par):
        t = tl["idx"]
        NJ = 4 * t + 4
        if t == 0:
            R.op("pool", lambda e: e.memset(CARU[:, 0, :], 0.0), writes=[r_caru])
            R.op("pool", lambda e: e.memset(QT[64:128, :, :], 0.0), writes=[r_qt])
            R.op("pool", lambda e: e.memset(QO[0:64, :, :], 0.0), writes=[r_qo])
            R.op("pool", lambda e: e.memset(KH[64:65, 0:4096], 1.0), writes=r_khs[0:4])
            R.op("pool", lambda e: e.memset(KH[32:33, 4096:8192], 1.0), writes=r_khs[4:8])
        else:
            R.op("dve", lambda e: e.tensor_copy(out=CARU[:, 0, :], in_=CARU[:, 4, :]), reads=[r_caru], writes=[r_caru])
        for u in range(4):
            cumsum_tile(LF[:, u, :], r_lf, 128, FH[:, 4 * t + u, :], u, u + 1)
        R.op("dve", lambda e: e.tensor_tensor(out=ZT[:, :, :], in0=FH[:, 4 * t:4 * t + 4, :], in1=bc(CARU[:, 4:5, :], 1, 4), op=ALU.subtract),
             reads=[r_fh, r_caru], writes=[r_zt])
        pt, rp = bankB()
        for u in range(4):
            R.op("pe", lambda e, u=u: e.transpose(pt[0:NH, u * 128:(u + 1) * 128], ZT[:, u, :], CST[:, 0:128]),
                 reads=[r_zt, r_cst], writes=[rp])
        R.op("dve", lambda e: e.tensor_copy(out=AQT[:, :], in_=pt[0:NH, :]), reads=[rp], writes=[r_aqt])
        for cc in range(8):
            R.dma("sp", lambda e, cc=cc: e.dma_start(out=QT[64:65, cc, :], in_=AQT[2 * cc:2 * cc + 1, :]), r_qt, reads=[r_aqt], writes=[r_qt])
            R.dma("sp", lambda e, cc=cc: e.dma_start(out=QO[32:33, cc, :], in_=AQT[2 * cc + 1:2 * cc + 2, :]), r_qo, reads=[r_aqt], writes=[r_qo])
        kt, rkt = KT[0]
        va, rva = VA[par]
        R.dma("sp", lambda e: e.dma_start(out=khist[:, :, t * 512:(t + 1) * 512].rearrange("c p k -> p c k"), in_=kt[:, :, :]),
              rkt, reads=[rkt], writes=[r_khist])
        R.dma("sp", lambda e: e.dma_start(out=vhist[:, :, 4 * t:4 * t + 4, :].rearrange("h p j e -> p h j e"),
                                          in_=va[:, :, :, :]),
              rva, reads=[rva], writes=[r_vhist])
        NBK = (NJ + 7) // 8
        for h in range(NH):
            c = h // 2
            pb = (h % 2) * 64
            orow = 64 if h % 2 == 0 else 63
            vslot = {}
            kslot = {}
            for b in range(NBK):
                j0, j1 = b * 8, min(NJ, b * 8 + 8)
                rk_ = "ke" if h % 2 == 0 else "ko"
                sk = (ring[rk_] % 4) + (0 if h % 2 == 0 else 4)
                ring[rk_] += 1
                kslot[b] = sk
                R.dma("sp", lambda e, c=c, pb=pb, sk=sk, j0=j0, j1=j1: e.dma_start(
                    out=KH[pb:pb + 64, sk * 1024:sk * 1024 + (j1 - j0) * 128], in_=khist[c, pb:pb + 64, j0 * 128:j1 * 128]),
                    r_khs[sk], reads=[r_khist], writes=[r_khs[sk]])
                sv = ring["v"] % 8
                ring["v"] += 1
                vslot[b] = sv
                R.dma("sp", lambda e, h=h, sv=sv, j0=j0, j1=j1: e.dma_start(
                    out=VH[:, sv * 8:sv * 8 + (j1 - j0), :], in_=vhist[h, :, j0:j1, :]),
                    r_vhs[sv], reads=[r_vhist], writes=[r_vhs[sv]])
            R.op("dve", lambda e, h=h: e.tensor_tensor(
                out=BI[:, 0, 0:NJ], in0=bc(CARU[:, 4, h:h + 1], 1, NJ), in1=FH[:, 0:NJ, h], op=ALU.subtract),
                reads=[r_caru, r_fh], writes=[r_bi])
            kts = []
            for j in range(NJ):
                d = j - 4 * t
                u0 = max(0, d)
                if d < 0:
                    groups = [(0, 512, BI[:, 0, j:j + 1], None)]
                else:
                    groups = [(d * 128, (d + 1) * 128, BI[:, 0, j:j + 1], MNEGD)]
                    if d < 3:
                        groups.append(((d + 1) * 128, 512, BI[:, 0, j:j + 1], None))
                sk = kslot[j // 8]
                sv = vslot[j // 8]
                kts.append(dict(k=KH[:, sk * 1024 + (j % 8) * 128:sk * 1024 + (j % 8 + 1) * 128], rk=r_khs[sk],
                                v=VH[:, sv * 8 + j % 8, :], rv=r_vhs[sv], nk=128,
                                c0=u0 * 128, c1=512, groups=groups, mul=None))
            if h % 2 == 0:
                attend(h, kts, 0, 512, qap=QT[:, c, :], rq=r_qt)
            else:
                attend(h, kts, 0, 512, qap=QO[:, c, :], rq=r_qo)

    def attn_B_sample(l, tl, par):
        ib = l // 2
        for b in range(2):
            prep_cache_K(cbk, ib, b, 8)
            R.dma("sp", lambda e, b=b: e.dma_start(out=LFC[:, :, :], in_=cbl[ib, b, :, :].rearrange("(j p) h -> p j h", p=128)),
                  r_lfc, writes=[r_lfc])
            R.op("pool", lambda e: e.memset(CARU[:, 0, :], 0.0), writes=[r_caru])
            for j in range(8):
                cumsum_tile(LFC[:, j, :], r_lfc, 128, FH[:, j, :], j % 2, (j + 1) % 2)
            cumsum_tile(LF[0:32, b, :], r_lf, 32, FH[0:32, 8, :], 0, 1)
            for h in range(NH):
                c = h // 2
                pb = (h % 2) * 64
                load_cache_V(cbv, ib, b, h, 8)
                R.op("dve", lambda e, h=h: e.tensor_tensor(
                    out=BI[:, 0, 0:8], in0=bc(CARU[:, 1, h:h + 1], 1, 8), in1=FH[:, 0:8, h], op=ALU.subtract),
                    reads=[r_caru, r_fh], writes=[r_bi])
                R.op("dve", lambda e, h=h: e.tensor_tensor(
                    out=BI[0:32, 0, 8:9], in0=CARU[0:32, 1, h:h + 1], in1=FH[0:32, 8, h:h + 1], op=ALU.subtract),
                    reads=[r_caru, r_fh], writes=[r_bi])
                kts = []
                for j in range(8):
                    kts.append(dict(k=KC[pb:pb + 64, c, j * 128:(j + 1) * 128], rk=r_kc, v=VH[:, j, :], rv=r_vh, nk=128,
                                    c0=0, c1=32, groups=[(0, 32, BI[:, 0, j:j + 1], None)], mul=None))
                kts.append(dict(k=KT[0][0][pb:pb + 64, c, b * 32:(b + 1) * 32], rk=KT[0][1],
                                v=VA[par][0][0:32, h, b, :], rv=VA[par][1], nk=32, c0=0, c1=32,
                                groups=[(0, 32, BI[0:32, 0, 8:9], MNEGD[0:32, 0:32])], mul=None))
                attend(h, kts, b * 32, 32)

    def residual_add(tl, u, half, pt, rp, w):
        rows = tl["rows"]
        b = 0 if tl["kind"] == "p" else u
        tmp, rt = TMP[(2 * u + half) % 2]
        R.op("dve", lambda e: e.tensor_tensor(out=tmp[0:rows, :], in0=pt[0:rows, :], in1=GA[0:rows, w, b, half * 512:(half + 1) * 512], op=ALU.mult),
             reads=[rp, r_ga], writes=[rt])
        R.op("pool", lambda e: e.tensor_tensor(out=XT[0:rows, u, half * 512:(half + 1) * 512], in0=XT[0:rows, u, half * 512:(half + 1) * 512],
                                               in1=tmp[0:rows, :], op=ALU.add), reads=[rt, r_xt], writes=[r_xt])

    def oproj(l, tl):
        nsub, rows = tl["nsub"], tl["rows"]
        for half in range(2):
            slot, rs = wload("o", l, wo_b, 0, 8, half * 512)
            for u in range(nsub):
                pt, rp = bankB()
                for kc in range(8):
                    R.op("pe", lambda e, kc=kc, pt=pt, u=u, slot=slot: e.matmul(
                        pt[0:rows, :], lhsT=OT[:, kc, u * rows:(u + 1) * rows], rhs=slot[:, kc, :], start=(kc == 0), stop=(kc == 7)),
                        reads=[rs, r_ot], writes=[rp])
                residual_add(tl, u, half, pt, rp, 0)

    def ffn(l, tl):
        T, nsub, rows = tl["T"], tl["nsub"], tl["rows"]
        for blk in range(11):
            slot, rs = wload("gu", l, wgu_b, 0, 8, blk * 512)
            for q4 in range(4):
                col = blk * 4 + q4
                pt, rp = bankA()
                for kc in range(8):
                    R.op("pe", lambda e, kc=kc, pt=pt, q4=q4, slot=slot: e.matmul(
                        pt[:, 0:T], lhsT=slot[:, kc, q4 * 128:(q4 + 1) * 128], rhs=HT[:, kc, 0:T], start=(kc == 0), stop=(kc == 7)),
                        reads=[rs, r_ht], writes=[rp])
                if col < 22:
                    R.op("act", lambda e, pt=pt, col=col: e.activation(out=AT[:, col, 0:T], in_=pt[:, 0:T], func=AF.Silu),
                         reads=[rp], writes=[r_at])
                else:
                    m = col - 22
                    R.op("dve", lambda e, pt=pt, m=m: e.tensor_tensor(out=AT[:, m, 0:T], in0=pt[:, 0:T], in1=AT[:, m, 0:T], op=ALU.mult),
                         reads=[rp, r_at], writes=[r_at])
        for half in range(2):
            accs = [PS[4 + u] for u in range(nsub)]
            for gi, (m0, nm) in enumerate(((0, 8), (8, 8), (16, 6))):
                slot, rs = wload("dn", l, wdn_b, m0 * 128, nm, half * 512)
                for u in range(nsub):
                    pt, rp = accs[u]
                    for mm in range(nm):
                        R.op("pe", lambda e, mm=mm, pt=pt, u=u, slot=slot, m0=m0, gi=gi, nm=nm: e.matmul(
                            pt[0:rows, :], lhsT=AT[:, m0 + mm, u * rows:(u + 1) * rows], rhs=slot[:, mm, :],
                            start=(gi == 0 and mm == 0), stop=(gi == 2 and mm == nm - 1)), reads=[rs, r_at], writes=[rp])
            for u in range(nsub):
                residual_add(tl, u, half, accs[u][0], accs[u][1], 1)

    def final_out(tl):
        nsub, rows = tl["nsub"], tl["rows"]
        rstd_calc(tl)
        R.dma("sp", lambda e: e.dma_start(out=GOUT[:], in_=bass.AP(g_out.tensor, 0, [[0, 128], [1, D]])), r_gout, writes=[r_gout])
        for u in range(nsub):
            R.op("dve", lambda e, u=u: e.scalar_tensor_tensor(
                out=XT[0:rows, u, :], in0=XT[0:rows, u, :], scalar=SS[0:rows, 4 + u:5 + u], in1=GOUT[0:rows, :],
                op0=ALU.mult, op1=ALU.mult), reads=[r_xt, r_ss, r_gout], writes=[r_xt])
        if tl["kind"] == "p":
            dst = y_p[tl["row0"]:tl["row0"] + 512, :].rearrange("(u p) d -> p u d", p=128)
        else:
            dst = y_s.rearrange("(u p) d -> p u d", p=32)
        store_out(dst, XT[0:rows, 0:nsub, :], r_xt)

    try:
        for l in range(NL):
            isB = (l % 2 == 1)
            ck("pre-mod")
            modulation(l)
            ck("mod")
            if not isB:
                build_MT(l // 2)
                ck("MT")
            else:
                ib = l // 2
                R.dma("pool", lambda e, ib=ib: e.dma_start(out=WFG[:, :, :], in_=w_fg[ib, :, :].rearrange("(kc p) h -> p kc h", p=128)),
                      r_wfg, writes=[r_wfg])
                R.dma("sp", lambda e, ib=ib: e.dma_start(out=BFG[:, :], in_=bass.AP(b_fg.tensor, ib * NH, [[0, 128], [1, NH]])),
                      r_bfg, writes=[r_bfg])
            for tl in tiles:
                nsub, rows = tl["nsub"], tl["rows"]
                ti = tl["idx"]
                par = ti % 2
                if tl["kind"] == "s":
                    csbc_fill()
                    mod_ga(l, [(1, 0), (2, 1)])
                if l == 0:
                    src = xp[tl["row0"]:tl["row0"] + 512, :] if tl["kind"] == "p" else xs
                    rsrc = []
                else:
                    src = xscr[tl["row0"]:tl["row0"] + tl["T"], :]
                    rsrc = [r_xscr[ti]]
                R.dma("sp", lambda e, src=src, rows=rows, nsub=nsub: e.dma_start(
                    out=XT[0:rows, 0:nsub, :], in_=src.rearrange("(u p) d -> p u d", p=rows)), r_xt, reads=rsrc, writes=[r_xt])
                norm_to_hT(tl, 0)
                ck("norm1 l%d t%d" % (l, ti))
                need_ktok, kv_dst = qkv(l, tl, par)
                ck("qkv l%d t%d" % (l, ti))
                if isB:
                    logf_compute(l, tl)
                    ck("logf l%d t%d" % (l, ti))
                if tl["kind"] == "p":
                    if isB:
                        attn_B_prompt(l, tl, par)
                    else:
                        attn_A_prompt(tl, par)
                else:
                    if isB:
                        attn_B_sample(l, tl, par)
                    else:
                        attn_A_sample(l, tl, par)
                ck("attn l%d t%d" % (l, ti))
                oproj(l, tl)
                ck("oproj l%d t%d" % (l, ti))
                norm_to_hT(tl, 1)
                ffn(l, tl)
                ck("ffn l%d t%d" % (l, ti))
                if l == NL - 1:
                    final_out(tl)
                else:
                    R.dma("sp", lambda e, tl=tl, rows=rows, nsub=nsub: e.dma_start(
                        out=xscr[tl["row0"]:tl["row0"] + tl["T"], :].rearrange("(u p) d -> p u d", p=rows), in_=XT[0:rows, 0:nsub, :]),
                        r_xt, reads=[r_xt], writes=[r_xscr[ti]])
                ck("end l%d t%d" % (l, ti))
    except _Stop:
        pass

    R.wait_all("sp", [ro.last_w for ro in out_res] + [rr.last_w for rr in (r_xt, r_ht, r_at, r_ga, r_mt, r_gs) if rr.last_w is not None])
    print("instr counts", {e: len(R.ins[e]) for e in ENGS}, "dma sems", len(R.dma_sems), flush=True)
    R.replay()
    st.close()
    return nc


def make_consts():
    c = np.zeros((128, 640), np.float32)
    c[:, 0:128] = np.eye(128, dtype=np.float32)
    k = np.arange(128)[:, None]
    m = np.arange(128)[None, :]
    c[:, 128:256] = (k <= m).astype(np.float32)
    c[:, 256:384] = 1.0
    c[:, 384:512] = np.where(m >= k, 0.0, NEG).astype(np.float32)
    c[:, 512] = 1e-6
    return c


_CACHE = {}


def run(inputs, NPT=16, NL=DEPTH, ncores=NCORES):
    key = (NPT, NL)
    if key not in _CACHE:
        _CACHE[key] = build(NPT, NL)
    nc = _CACHE[key]
    f = lambda a: np.ascontiguousarray(np.asarray(a), dtype=np.float32)
    I = {k: f(v) for k, v in inputs.items()}
    cstv = make_consts()
    in_maps = []
    for i in range(ncores):
        m = {
            "xp": I["x_prompt"][i], "xs": I["x_sample"][2 * i:2 * i + 2].reshape(64, D),
            "cak": np.ascontiguousarray(I["cache_a_k"][:, 2 * i:2 * i + 2].reshape(2, 2, APAST, D)),
            "cav": np.ascontiguousarray(I["cache_a_v"][:, 2 * i:2 * i + 2].reshape(2, 2, APAST, D)),
            "cbk": np.ascontiguousarray(I["cache_b_k"][:, 2 * i:2 * i + 2].reshape(2, 2, PAST, D)),
            "cbv": np.ascontiguousarray(I["cache_b_v"][:, 2 * i:2 * i + 2].reshape(2, 2, PAST, D)),
            "cbl": np.ascontiguousarray(I["cache_b_logf"][:, 2 * i:2 * i + 2]),
            "c3": np.ascontiguousarray(np.concatenate([I["c_prompt"][i:i + 1], I["c_sample"][2 * i:2 * i + 2]], axis=0)),
            "w_mod": I["w_mod"], "b_mod": I["b_mod"], "g_mix": I["g_mix"], "g_ffn": I["g_ffn"],
            "w_qkv": I["w_qkv"], "w_o": I["w_o"], "rel_bias": I["rel_bias"], "w_fgate": I["w_fgate"],
            "b_fgate": I["b_fgate"], "w_gu": I["w_gu"], "w_down": I["w_down"], "g_out": I["g_out"].reshape(1, D),
            "cst": cstv,
        }
        in_maps.append(m)
    res = run_bass_kernel_spmd(nc, in_maps, core_ids=list(range(ncores)))
    return res.results


def kernel(**inputs):
    rs = run(inputs)
    y_p = np.stack([r["y_p"] for r in rs], 0)
    y_s = np.concatenate([r["y_s"].reshape(2, 32, D) for r in rs], 0)

    def pst(name, rows):
        return np.stack([r[name] for r in rs], 1).reshape(2, NCORES, rows, NH, HD)

    def sst(name):
        return np.concatenate([r[name].reshape(2, 2, 32, NH, HD) for r in rs], 1)

    pbl = np.stack([r["pbl"] for r in rs], 1)
    sbl = np.concatenate([r["sbl"].reshape(2, 2, 32, NH) for r in rs], 1)
    outs = (y_p, y_s, pst("pak", APAST), pst("pav", APAST), pst("pbk", SEQ), pst("pbv", SEQ), pbl,
            sst("sak"), sst("sav"), sst("sbk"), sst("sbv"), sbl)
    return tuple(np.ascontiguousarray(o, dtype=np.float32) for o in outs)
```

```python
import contextlib
import numpy as np
import concourse.bass as bass
import concourse.mybir as mybir
from concourse.bass_utils import run_bass_kernel_spmd

F32 = mybir.dt.float32
BF16 = mybir.dt.bfloat16
AF = mybir.ActivationFunctionType
ALU = mybir.AluOpType

D = 1024
DFF = 2816
NH = 16
HD = 64
SEQ = 8192
PAST = 1024
APAST = 512
DEPTH = 4
NCORES = 8
ENGS = ["pe", "act", "dve", "pool", "sp"]
NEG = -30000.0


class Res:
    __slots__ = ("name", "last_w", "readers", "sem", "semval", "excl")

    def __init__(self, name, excl=False):
        self.name = name
        self.excl = excl
        self.last_w = None
        self.readers = []
        self.sem = None
        self.semval = 0


class Rec:
    def __init__(self, nc):
        self.nc = nc
        self.ins = {e: [] for e in ENGS}
        self.dma_sems = []
        self.pool_dmas = []

    def _deps(self, reads, writes):
        deps = []
        for r in reads:
            if r.last_w is not None:
                deps.append(r.last_w)
            if r.excl:
                deps.extend((t[0], t[1], t[2], "x") if t[0] == "e" else t for t in r.readers)
        for w in writes:
            if w.last_w is not None:
                t = w.last_w
                deps.append((t[0], t[1], t[2], "x") if t[0] == "e" else t)
            deps.extend((t[0], t[1], t[2], "x") if t[0] == "e" else t for t in w.readers)
        return deps

    def op(self, eng, fn, reads=(), writes=(), relax=False):
        deps = self._deps(reads, writes)
        if not relax:
            deps = [d[:3] if d[0] == "e" else d for d in deps]
        idx = len(self.ins[eng])
        self.ins[eng].append([fn, deps, False, None])
        tok = ("e", eng, idx)
        for r in reads:
            r.readers.append(tok)
        for w in writes:
            w.last_w = tok
            w.readers = []
        return tok

    def dma(self, q, fn, semres, reads=(), writes=()):
        deps = self._deps(reads, writes)
        if semres.sem is None:
            semres.sem = {}
            semres.semval = {}
        if q not in semres.sem:
            semres.sem[q] = len(self.dma_sems)
            semres.semval[q] = 0
            self.dma_sems.append((semres, q))
        semres.semval[q] += 16
        tok = ("d", semres.sem[q], semres.semval[q])
        if q == "pool":
            self.pool_dmas.append(tok)
            if len(self.pool_dmas) > 2:
                deps = deps + [self.pool_dmas[-3]]
        self.ins[q].append([fn, deps, False, semres.sem[q]])
        for r in reads:
            r.readers.append(tok)
        for w in writes:
            w.last_w = tok
            w.readers = []
        return tok

    def wait_all(self, eng, toks):
        self.ins[eng].append([None, list(toks), False, None])

    def replay(self):
        nc = self.nc
        for e in ENGS:
            for it in self.ins[e]:
                for d in it[1]:
                    if d[0] == "e":
                        e2, i2 = d[1], d[2]
                        if e2 == e and (e in ("pe", "sp") or len(d) == 4):
                            continue
                        self.ins[e2][i2][2] = True
        cnt = {}
        for e in ENGS:
            c = 0
            for idx, it in enumerate(self.ins[e]):
                if it[2]:
                    c += 1
                cnt[(e, idx)] = c
        with contextlib.ExitStack() as st:
            esem = {e: st.enter_context(nc.semaphore("sem_" + e)) for e in ENGS}
            dsem = [st.enter_context(nc.semaphore("dsem%d" % i)) for i in range(len(self.dma_sems))]
            block = st.enter_context(nc.Block())

            def run(e, engobj):
                waited = {}
                for idx, it in enumerate(self.ins[e]):
                    need = {}
                    for d in it[1]:
                        if d[0] == "e":
                            e2, i2 = d[1], d[2]
                            if e2 == e and (e in ("pe", "sp") or len(d) == 4):
                                continue
                            key = ("e", e2)
                            v = cnt[(e2, i2)]
                            sem = esem[e2]
                        else:
                            _, si, v = d
                            key = ("d", si)
                            sem = dsem[si]
                        if waited.get(key, 0) >= v:
                            continue
                        if key not in need or need[key][1] < v:
                            need[key] = (sem, v)
                    for key, (sem, v) in need.items():
                        engobj.wait_ge(sem, v)
                        waited[key] = v
                    if it[0] is None:
                        continue
                    ins = it[0](engobj)
                    if it[3] is not None:
                        ins.then_inc(dsem[it[3]], 16)
                    elif it[2]:
                        ins.then_inc(esem[e], 1)

            block.tensor(lambda eng: run("pe", eng))
            block.scalar(lambda eng: run("act", eng))
            block.vector(lambda eng: run("dve", eng))
            block.gpsimd(lambda eng: run("pool", eng))
            block.sync(lambda eng: run("sp", eng))


def bc(ap, axis, n):
    a = [list(x) for x in ap.ap]
    assert a[axis][1] == 1, (a, axis)
    a[axis] = [0, n]
    return bass.AP(ap.tensor, ap.offset, a)


def build(NPT=16, NL=DEPTH):
    nc = bass.Bass("TRN2", target_bir_lowering=False)
    R = Rec(nc)

    def din(name, shape, dt=F32):
        return nc.dram_tensor(name, list(shape), dt, kind="ExternalInput").ap()

    def dout(name, shape, dt=F32):
        return nc.dram_tensor(name, list(shape), dt, kind="ExternalOutput").ap()

    def dscr(name, shape, dt):
        return nc.dram_tensor(name, list(shape), dt).ap()

    xp = din("xp", [SEQ, D])
    xs = din("xs", [64, D])
    cak = din("cak", [2, 2, APAST, D])
    cav = din("cav", [2, 2, APAST, D])
    cbk = din("cbk", [2, 2, PAST, D])
    cbv = din("cbv", [2, 2, PAST, D])
    cbl = din("cbl", [2, 2, PAST, NH])
    c3 = din("c3", [3, D])
    w_mod = din("w_mod", [DEPTH, D, 6 * D])
    b_mod = din("b_mod", [DEPTH, 6 * D])
    g_mix = din("g_mix", [DEPTH, D])
    g_ffn = din("g_ffn", [DEPTH, D])
    w_qkv = din("w_qkv", [DEPTH, D, 3 * D])
    w_o = din("w_o", [DEPTH, D, D])
    rel_bias = din("rel_bias", [2, NH, 320])
    w_fg = din("w_fgate", [2, D, NH])
    b_fg = din("b_fgate", [2, NH])
    w_gu = din("w_gu", [DEPTH, D, 2 * DFF])
    w_dn = din("w_down", [DEPTH, DFF, D])
    g_out = din("g_out", [1, D])
    cst = din("cst", [128, 640])

    y_p = dout("y_p", [SEQ, D])
    y_s = dout("y_s", [64, D])
    o_pak = dout("pak", [2, APAST, D])
    o_pav = dout("pav", [2, APAST, D])
    o_pbk = dout("pbk", [2, SEQ, D])
    o_pbv = dout("pbv", [2, SEQ, D])
    o_pbl = dout("pbl", [2, SEQ, NH])
    o_sak = dout("sak", [2, 64, D])
    o_sav = dout("sav", [2, 64, D])
    o_sbk = dout("sbk", [2, 64, D])
    o_sbv = dout("sbv", [2, 64, D])
    o_sbl = dout("sbl", [2, 64, NH])

    wmod_b = dscr("wmod_b", [DEPTH, 12, 128, 8, 512], BF16)
    wqkv_b = dscr("wqkv_b", [DEPTH, 6, 128, 8, 512], BF16)
    wo_b = dscr("wo_b", [DEPTH, 2, 128, 8, 512], BF16)
    wgu_b = dscr("wgu_b", [DEPTH, 11, 128, 8, 512], BF16)
    wdn_b = dscr("wdn_b", [DEPTH, 2, 128, 22, 512], BF16)
    xscr = dscr("xscr", [SEQ + 64, D], F32)
    khist = dscr("khist", [8, 128, SEQ], BF16)
    vhist = dscr("vhist", [NH, 128, SEQ // 128, 128], BF16)
    etab = dscr("etab", [NH, 768], F32)
    ones_d = dscr("ones_d", [1, 1024], BF16)
    r_onesd = Res("ones_d")
    etab_rep = dscr("etab_rep", [NH, 128, 768], F32)

    r_w = {}
    for l in range(DEPTH):
        for nm in ("mod", "qkv", "o", "gu", "dn"):
            r_w[(nm, l)] = Res("w%s%d" % (nm, l))
    r_xscr = [Res("xscr%d" % t) for t in range(NPT + 1)]
    r_khist = Res("khist")
    r_vhist = Res("vhist")
    r_etab = Res("etab")
    r_etabrep = Res("etabrep")
    out_res = []

    st = contextlib.ExitStack()
    r_khs = [Res("khs%d" % i) for i in range(8)]
    r_vhs = [Res("vhs%d" % i) for i in range(8)]

    def sb(name, shape, dt):
        t = st.enter_context(nc.sbuf_tensor(name, list(shape), dt))
        return t, Res(name)

    CST, r_cst = sb("CST", [128, 640], F32)
    IDB, r_idb = sb("IDB", [128, 128], BF16)
    XT, r_xt = sb("XT", [128, 4, D], F32)
    XN = [sb("XN0", [128, D], BF16)] * 2
    SS, r_ss = sb("SS", [128, 8], F32)
    HT, r_ht = sb("HT", [128, 8, 512], BF16)
    QT, r_qt = sb("QT", [128, 8, 512], BF16)
    KT = [sb("KT%d" % i, [128, 8, 512], BF16) for i in range(2)]
    VA = [sb("VA%d" % i, [128, NH, 4, 128], BF16) for i in range(2)]
    STK = [sb("STK0", [128, D], F32)] * 2
    STV = [sb("STV%d" % i, [128, 512], F32) for i in range(2)]
    OT, r_ot = HT, r_ht
    NPT_BUF = 3
    PT = [sb("PT%d" % i, [128, 512], BF16) for i in range(NPT_BUF)]
    SX = [sb("SX0", [128, 128], F32)] * 2
    RC = [sb("RC%d" % i, [128, 512], F32) for i in range(2)]
    TMP = RC
    NSLOT = 3
    WS = [sb("WS%d" % i, [128, 8, 512], BF16) for i in range(NSLOT)]
    AT, r_at = sb("AT", [128, 22, 512], BF16)
    SQ, r_sq = AT[:, 20:22, :].rearrange("p a n -> p (a n)"), r_at
    CT, r_ct = sb("CT", [128, 8, 3], F32)
    CSB, r_csb = sb("CSB", [128, 8, 3], BF16)
    CSBC, r_csbc = AT[:, 0:6, :].rearrange("p a (b n) -> p (a b) n", n=128).rearrange("p (kc b) n -> p kc b n", b=3), r_at
    MODT, r_modt = sb("MODT", [128, 4, 8, 3], F32)
    BMT, r_bmt = sb("BMT", [128, 48], F32)
    GMT, r_gmt = sb("GMT", [128, 2, 8], F32)
    GS, r_gs = sb("GS", [128, 2, 8, 3], F32)
    SH, r_sh = sb("SH", [128, 2, 8, 3], F32)
    GA, r_ga = sb("GA", [128, 2, 2, D], F32)
    GOUT, r_gout = STK[0]
    LF, r_lf = sb("LF", [128, 4, NH], F32)
    AQT128, r_aqt = sb("AQT", [128, 512], BF16)
    AQT = AQT128[0:NH, :]
    PT.append((AQT128, r_aqt))
    ZT, r_zt = sb("ZT", [128, 4, NH], F32)
    BFG, r_bfg = sb("BFG", [128, NH], F32)
    WFG, r_wfg = sb("WFG", [128, 8, NH], BF16)
    FH, r_fh = sb("FH", [128, 64, NH], F32)
    CARU, r_caru = sb("CARU", [128, 5, NH], F32)
    BI, r_bi = sb("BI", [128, 4, 64], F32)
    KH, r_kh = sb("KH", [128, SEQ], BF16)
    VH, r_vh = AT[:, 0:16, :].rearrange("p a n -> p (a n)").rearrange("p (j e) -> p j e", e=128), r_at
    KC, r_kc = KH[:, :].rearrange("p (c k) -> p c k", k=PAST), r_kh
    MT, r_mt = sb("MT", [128, NH, 5, 128], BF16)
    ETS, r_ets = XT[0:NH, 0, 0:768], r_xt
    CKS = XN
    LFC, r_lfc = sb("LFC", [128, 8, NH], F32)

    PS = []
    for i in range(8):
        t = st.enter_context(nc.psum_tensor("ps%d" % i, [128, 512], F32))
        PS.append((t, Res("ps%d" % i, excl=True)))
    rot = {"A": 0, "B": 0}

    def bankA():
        rot["A"] = (rot["A"] + 1) % 4
        return PS[rot["A"]]

    def bankB():
        rot["B"] = (rot["B"] + 1) % 4
        return PS[4 + rot["B"]]

    R.dma("sp", lambda e: e.dma_start(out=CST[:], in_=cst), r_cst, writes=[r_cst])
    UTRI = CST[:, 128:256]
    ONESF = CST[:, 256:384]
    MNEGD = CST[:, 384:512]
    EPSC = CST[:, 512:513]
    R.op("dve", lambda e: e.tensor_copy(out=IDB[:], in_=CST[:, 0:128]), reads=[r_cst], writes=[r_idb])
    for i in range(2):
        R.op("pool", lambda e, i=i: e.memset(VA[i][0][:, :, :, 64:128], 1.0), writes=[VA[i][1]])
    TRS, r_trs = sb("TRS", [64, 128], F32)

    def load_T(pieces, dsts):
        for ap, r0, n in pieces:
            R.dma("sp", lambda e, ap=ap, r0=r0, n=n: e.dma_start(out=TRS[r0:r0 + n, :], in_=ap), r_trs, writes=[r_trs])
        nrow = max(r0 + n for _, r0, n in pieces)
        pt, rp = bankA()
        R.op("pe", lambda e: e.transpose(pt[:, 0:nrow], TRS[0:nrow, :], CST[0:nrow, 0:nrow]), reads=[r_trs, r_cst], writes=[rp])
        for dst, rdst, c0, c1 in dsts:
            R.op("dve", lambda e, dst=dst, c0=c0, c1=c1: e.tensor_copy(out=dst, in_=pt[:, c0:c1]), reads=[rp], writes=[rdst])

    load_T([(c3.rearrange("b (kc p) -> (b kc) p", p=128), 0, 24)],
           [(CT[:, :, :].rearrange("p kc b -> p b kc"), r_ct, 0, 24)])
    R.op("act", lambda e: e.activation(out=CSB[:], in_=CT[:], func=AF.Silu), reads=[r_ct], writes=[r_csb])
    R.op("pool", lambda e: e.memset(KH[:, :], 0.0), writes=[r_kh] + r_khs)
    R.op("pool", lambda e: e.memset(PT[0][0][0:1, :], 1.0), writes=[PT[0][1]])
    for hh in range(2):
        R.dma("sp", lambda e, hh=hh: e.dma_start(out=ones_d[0:1, hh * 512:(hh + 1) * 512], in_=PT[0][0][0:1, :]),
              PT[0][1], reads=[PT[0][1]], writes=[r_onesd])
    QO, r_qo = KT[1]

    def cast_w(nm, l, src, dst, K, N):
        res = r_w[(nm, l)]
        for kc in range(K // 128):
            s_ap = src[l, kc * 128:(kc + 1) * 128, :].rearrange("p (j n) -> p j n", n=512)
            d_ap = dst[l, :, :, kc, :].rearrange("j p n -> p j n")
            R.dma("pool", lambda e, s_ap=s_ap, d_ap=d_ap: e.dma_start(out=d_ap, in_=s_ap), res, writes=[res])

    for l in range(NL):
        cast_w("mod", l, w_mod, wmod_b, D, 6 * D)
        cast_w("qkv", l, w_qkv, wqkv_b, D, 3 * D)
        cast_w("o", l, w_o, wo_b, D, D)
        cast_w("gu", l, w_gu, wgu_b, D, 2 * DFF)
        cast_w("dn", l, w_dn, wdn_b, DFF, D)

    wstate = {"n": 0}
    ring = {"k": 0, "v": 0, "ke": 0, "ko": 0}

    def wload(nm, l, wt, r0, nkc, c0, ncol=512):
        slot, rs = WS[wstate["n"] % NSLOT]
        wstate["n"] += 1
        src = wt[l, c0 // 512, :, r0 // 128:r0 // 128 + nkc, :]
        R.dma("sp", lambda e: e.dma_start(out=slot[:, 0:nkc, 0:ncol], in_=src), rs, reads=[r_w[(nm, l)]], writes=[rs])
        return slot, rs

    import os as _os
    _lim = int(_os.environ.get("DBG_STOP", "1000000"))
    _cnt = [0]

    class _Stop(Exception):
        pass

    def ck(name):
        _cnt[0] += 1
        if _cnt[0] >= _lim:
            print("DBG_STOP at", _cnt[0], name, flush=True)
            raise _Stop()

    tiles = []
    for t in range(NPT):
        tiles.append(dict(kind="p", idx=t, T=512, nsub=4, rows=128, row0=t * 512))
    tiles.append(dict(kind="s", idx=NPT, T=64, nsub=2, rows=32, row0=SEQ))

    def store_out(dst_ap, src_ap, rsrc):
        ro = Res("o")
        out_res.append(ro)
        R.dma("sp", lambda e: e.dma_start(out=dst_ap, in_=src_ap), rsrc, reads=[rsrc], writes=[ro])

    def mod_ga(l, pairs):
        for w, part in enumerate((2, 5)):
            for (b, sl) in pairs:
                R.dma("sp", lambda e, w=w, part=part, sl=sl: e.dma_start(
                    out=GA[:, w, sl, :], in_=bass.AP(b_mod.tensor, l * 6 * D + part * D, [[0, 128], [1, D]])),
                    r_ga, writes=[r_ga])
            for half in range(2):
                slot, rs = wload("mod", l, wmod_b, 0, 8, part * 1024 + half * 512)
                for (b, sl) in pairs:
                    pt, rp = bankB()
                    for kc in range(8):
                        R.op("pe", lambda e, kc=kc, pt=pt, b=b, slot=slot: e.matmul(
                            pt[:, :], lhsT=CSBC[:, kc, b, :], rhs=slot[:, kc, :], start=(kc == 0), stop=(kc == 7)),
                            reads=[rs, r_csbc], writes=[rp])
                    R.op("dve", lambda e, pt=pt, sl=sl, w=w, half=half: e.tensor_tensor(
                        out=GA[:, w, sl, half * 512:(half + 1) * 512], in0=pt[:, :],
                        in1=GA[:, w, sl, half * 512:(half + 1) * 512], op=ALU.add),
                        reads=[rp, r_ga], writes=[r_ga])

    def csbc_fill():
        for kc in range(8):
            for b in range(3):
                R.op("dve", lambda e, kc=kc, b=b: e.tensor_copy(out=CSBC[:, kc, b, :], in_=bc(CSB[:, kc, b:b + 1], 1, 128)),
                     reads=[r_csb], writes=[r_csbc])

    def modulation(l):
        load_T([(b_mod[l, :].rearrange("(o p) -> o p", p=128), 0, 48),
                (g_mix[l, :].rearrange("(o p) -> o p", p=128), 48, 8),
                (g_ffn[l, :].rearrange("(o p) -> o p", p=128), 56, 8)],
               [(BMT[:, :], r_bmt, 0, 48), (GMT[:, :, :].rearrange("p w o -> p (w o)"), r_gmt, 48, 64)])
        for cb in range(12):
            part = cb // 2
            half = cb % 2
            if part in (2, 5):
                continue
            slot, rs = wload("mod", l, wmod_b, 0, 8, cb * 512)
            pi = {0: 0, 1: 1, 3: 2, 4: 3}[part]
            for q4 in range(4):
                oc = half * 4 + q4
                pt, rp = bankA()
                for kc in range(8):
                    R.op("pe", lambda e, kc=kc, pt=pt, q4=q4, slot=slot: e.matmul(
                        pt[:, 0:3], lhsT=slot[:, kc, q4 * 128:(q4 + 1) * 128], rhs=CSB[:, kc, :],
                        start=(kc == 0), stop=(kc == 7)), reads=[rs, r_csb], writes=[rp])
                col = part * 8 + oc
                R.op("dve", lambda e, pt=pt, pi=pi, oc=oc, col=col: e.tensor_scalar(
                    out=MODT[:, pi, oc, :], in0=pt[:, 0:3], scalar1=BMT[:, col:col + 1], scalar2=None, op0=ALU.add),
                    reads=[rp, r_bmt], writes=[r_modt])
        for w in range(2):
            for b in range(3):
                R.op("dve", lambda e, w=w, b=b: e.scalar_tensor_tensor(
                    out=GS[:, w, :, b], in0=MODT[:, 2 * w + 1, :, b], scalar=1.0, in1=GMT[:, w, :],
                    op0=ALU.add, op1=ALU.mult), reads=[r_modt, r_gmt], writes=[r_gs])
                R.op("dve", lambda e, w=w, b=b: e.tensor_copy(out=SH[:, w, :, b], in_=MODT[:, 2 * w, :, b]),
                     reads=[r_modt], writes=[r_sh])
        csbc_fill()
        mod_ga(l, [(0, 0)])

    def rstd_calc(tl):
        nsub, rows = tl["nsub"], tl["rows"]
        for u in range(nsub):
            R.op("act", lambda e, u=u: e.activation(out=SQ[0:rows, :], in_=XT[0:rows, u, :], func=AF.Square,
                                                    accum_out=SS[0:rows, u:u + 1]),
                 reads=[r_xt], writes=[r_sq, r_ss])
        R.op("act", lambda e: e.activation(out=SS[0:rows, 4:4 + nsub], in_=SS[0:rows, 0:nsub], func=AF.Sqrt,
                                           scale=1.0 / D, bias=EPSC[0:rows, :]), reads=[r_ss, r_cst], writes=[r_ss])
        R.op("dve", lambda e: e.reciprocal(out=SS[0:rows, 4:4 + nsub], in_=SS[0:rows, 4:4 + nsub]), reads=[r_ss], writes=[r_ss])

    def norm_to_hT(tl, w):
        nsub, rows = tl["nsub"], tl["rows"]
        rstd_calc(tl)
        for u in range(nsub):
            xn, rxn = XN[u % 2]
            R.op("act", lambda e, u=u, xn=xn: e.activation(out=xn[0:rows, :], in_=XT[0:rows, u, :], func=AF.Copy,
                                                           scale=SS[0:rows, 4 + u:5 + u]), reads=[r_xt, r_ss], writes=[rxn])
            for half in range(2):
                pt, rp = bankA()
                ptb = pt[:].bitcast(BF16)
                for c4 in range(4):
                    c = half * 4 + c4
                    R.op("pe", lambda e, c=c, c4=c4, ptb=ptb, xn=xn: e.transpose(
                        ptb[:, c4 * 128:c4 * 128 + rows], xn[0:rows, c * 128:(c + 1) * 128], IDB[0:rows, 0:rows]),
                        reads=[rxn, r_idb], writes=[rp])
                b = 0 if tl["kind"] == "p" else 1 + u
                for c4 in range(4):
                    c = half * 4 + c4
                    R.op("dve", lambda e, c=c, c4=c4, ptb=ptb, u=u, b=b: e.tensor_scalar(
                        out=HT[:, c, u * rows:(u + 1) * rows], in0=ptb[:, c4 * 128:c4 * 128 + rows],
                        scalar1=GS[:, w, c, b:b + 1], scalar2=SH[:, w, c, b:b + 1], op0=ALU.mult, op1=ALU.add),
                        reads=[rp, r_gs, r_sh], writes=[r_ht])

    def qkv(l, tl, par):
        T, nsub, rows = tl["T"], tl["nsub"], tl["rows"]
        isB = (l % 2 == 1)
        li = l // 2
        kt, rkt = KT[0 if isB else par]
        va, rva = VA[par]
        splitq = isB and tl["kind"] == "p"
        last_p = (tl["kind"] == "p" and tl["idx"] == NPT - 1)
        need_ktok = isB or last_p or tl["kind"] == "s"
        need_vout = need_ktok

        def kv_dst(which, u):
            if tl["kind"] == "s":
                o = {("k", 0): o_sak, ("v", 0): o_sav, ("k", 1): o_sbk, ("v", 1): o_sbv}[(which, l % 2)]
                return o[li, u * 32:(u + 1) * 32, :]
            if isB:
                o = o_pbk if which == "k" else o_pbv
                return o[li, tl["row0"] + u * 128: tl["row0"] + (u + 1) * 128, :]
            o = o_pak if which == "k" else o_pav
            return o[li, u * 128:(u + 1) * 128, :]

        def feat_major(blk, slot, rs):
            for q4 in range(4):
                oc = (blk % 2) * 4 + q4
                pt, rp = bankA()
                for kc in range(8):
                    R.op("pe", lambda e, kc=kc, pt=pt, q4=q4, slot=slot: e.matmul(
                        pt[:, 0:T], lhsT=slot[:, kc, q4 * 128:(q4 + 1) * 128], rhs=HT[:, kc, 0:T],
                        start=(kc == 0), stop=(kc == 7)), reads=[rs, r_ht], writes=[rp])
                if blk < 2 and splitq:
                    R.op("act", lambda e, pt=pt, oc=oc: e.activation(out=QT[0:64, oc, 0:T], in_=pt[0:64, 0:T], func=AF.Copy, scale=0.125),
                         reads=[rp], writes=[r_qt])
                    R.op("act", lambda e, pt=pt, oc=oc: e.activation(out=QO[64:128, oc, 0:T], in_=pt[64:128, 0:T], func=AF.Copy, scale=0.125),
                         reads=[rp], writes=[r_qo])
                elif blk < 2:
                    R.op("act", lambda e, pt=pt, oc=oc: e.activation(out=QT[:, oc, 0:T], in_=pt[:, 0:T], func=AF.Copy, scale=0.125),
                         reads=[rp], writes=[r_qt])
                else:
                    R.op("dve", lambda e, pt=pt, oc=oc: e.tensor_copy(out=kt[:, oc, 0:T], in_=pt[:, 0:T]), reads=[rp], writes=[rkt])

        for blk in (0, 1):
            slot, rs = wload("qkv", l, wqkv_b, 0, 8, blk * 512)
            feat_major(blk, slot, rs)
        kslots = [wload("qkv", l, wqkv_b, 0, 8, blk * 512) for blk in (2, 3)]
        for i, blk in enumerate((2, 3)):
            feat_major(blk, kslots[i][0], kslots[i][1])
        ck("qkv-qk")
        if need_ktok:
            for u in range(nsub):
                stg, rstg = STK[u % 2]
                for half in range(2):
                    slot, rs = kslots[half]
                    pt, rp = bankB()
                    for kc in range(8):
                        R.op("pe", lambda e, kc=kc, pt=pt, u=u, slot=slot: e.matmul(
                            pt[0:rows, :], lhsT=HT[:, kc, u * rows:(u + 1) * rows], rhs=slot[:, kc, :],
                            start=(kc == 0), stop=(kc == 7)), reads=[rs, r_ht], writes=[rp])
                    R.op("dve", lambda e, pt=pt, stg=stg, half=half: e.tensor_copy(out=stg[0:rows, half * 512:(half + 1) * 512], in_=pt[0:rows, :]),
                         reads=[rp], writes=[rstg])
                ck("ktok-mm u%d" % u)
                store_out(kv_dst("k", u), stg[0:rows, :], rstg)
                ck("ktok-store u%d" % u)
        for blk in (4, 5):
            slot, rs = wload("qkv", l, wqkv_b, 0, 8, blk * 512)
            half = blk - 4
            for u in range(nsub):
                pt, rp = bankB()
                for kc in range(8):
                    R.op("pe", lambda e, kc=kc, pt=pt, u=u, slot=slot: e.matmul(
                        pt[0:rows, :], lhsT=HT[:, kc, u * rows:(u + 1) * rows], rhs=slot[:, kc, :],
                        start=(kc == 0), stop=(kc == 7)), reads=[rs, r_ht], writes=[rp])
                R.op("act", lambda e, pt=pt, u=u, half=half: e.activation(
                    out=va[0:rows, half * 8:(half + 1) * 8, u, 0:64],
                    in_=pt[0:rows, :].rearrange("p (h e) -> p h e", e=64), func=AF.Copy),
                    reads=[rp], writes=[rva])
                if need_vout:
                    stg, rstg = STV[(2 * half + u) % 2]
                    R.op("dve", lambda e, pt=pt, stg=stg: e.tensor_copy(out=stg[0:rows, 0:512], in_=pt[0:rows, :]),
                         reads=[rp], writes=[rstg])
                    store_out(kv_dst("v", u)[:, half * 512:(half + 1) * 512], stg[0:rows, 0:512], rstg)
        return need_ktok, kv_dst

    def k_tokmajor(l, tl):
        pass

    def normalize_head(h, po, rpo, q0, nq):
        rc, rrc = RC[h % 2]
        c = h // 2
        pb = (h % 2) * 64
        R.op("dve", lambda e: e.reciprocal(out=rc[64:128, q0:q0 + nq], in_=po[64:128, q0:q0 + nq]), reads=[rpo], writes=[rrc])
        R.op("dve", lambda e: e.tensor_tensor(out=OT[pb:pb + 64, c, q0:q0 + nq], in0=po[0:64, q0:q0 + nq],
                                              in1=rc[64:128, q0:q0 + nq], op=ALU.mult),
             reads=[rpo, rrc], writes=[r_ot])

    def attend(h, ktiles, qoff, nq, exact_recip=True, qap=None, rq=None, nbuf=3):
        c = h // 2
        pb = (h % 2) * 64
        po, rpo = PS[4 + (h % 3)]
        pend = []
        if qap is None:
            qap, rq = QT[pb:pb + 64, c, :], r_qt

        def emit_pv(ktl, pT, rpT, first):
            nk, c0, c1 = ktl["nk"], ktl["c0"], ktl["c1"]
            R.op("pe", lambda e: e.matmul(
                po[:, c0:c1], lhsT=ktl["v"], rhs=pT[0:nk, c0:c1], start=first, stop=False, skip_group_check=True),
                reads=[ktl["rv"], rpT], writes=[rpo])

        for i, ktl in enumerate(ktiles):
            nk, c0, c1 = ktl["nk"], ktl["c0"], ktl["c1"]
            pt, rp = bankA()
            pT, rpT = PT[i % nbuf]
            has_lm = ktl.get("mul") is not None
            R.op("pe", lambda e, ktl=ktl, pt=pt, nk=nk, c0=c0, c1=c1, has_lm=has_lm: e.matmul(
                pt[0:nk, c0:c1], lhsT=ktl["k"], rhs=qap[:, qoff + c0:qoff + c1], start=True, stop=not has_lm),
                reads=[ktl["rk"], rq], writes=[rp])
            if has_lm:
                lm = ktl["mul"].rearrange("p u q -> p (u q)")
                R.op("pe", lambda e, pt=pt, nk=nk, c0=c0, c1=c1, lm=lm: e.matmul(
                    pt[0:nk, c0:c1], lhsT=IDB[0:nk, 0:nk], rhs=lm, start=False, stop=True),
                    reads=[r_mt, r_idb], writes=[rp])
            if len(pend) >= nbuf - 1:
                emit_pv(*pend.pop(0))
            for gi, (g0, g1, bias, mneg) in enumerate(ktl["groups"]):
                src = pt[0:nk, g0:g1]
                rsrc = rp
                if mneg is not None:
                    sx, rsx = SX[gi % 2]
                    R.op("dve", lambda e, sx=sx, src=src, mneg=mneg, nk=nk, g0=g0, g1=g1: e.tensor_tensor(
                        out=sx[0:nk, 0:g1 - g0], in0=src, in1=mneg, op=ALU.add), reads=[rp, r_cst], writes=[rsx])
                    src = sx[0:nk, 0:g1 - g0]
                    rsrc = rsx
                if bias is not None:
                    R.op("act", lambda e, src=src, pT=pT, nk=nk, g0=g0, g1=g1, bias=bias: e.activation(
                        out=pT[0:nk, g0:g1], in_=src, func=AF.Exp, bias=bias), reads=[rsrc, r_bi], writes=[rpT], relax=True)
                else:
                    R.op("act", lambda e, src=src, pT=pT, nk=nk, g0=g0, g1=g1: e.activation(
                        out=pT[0:nk, g0:g1], in_=src, func=AF.Exp), reads=[rsrc], writes=[rpT], relax=True)
            pend.append((ktl, pT, rpT, i == 0))
        while pend:
            emit_pv(*pend.pop(0))
        normalize_head_off(h, po, rpo, qoff, nq, exact_recip)

    def normalize_head_off(h, po, rpo, qoff, nq, exact_recip=True):
        rc, rrc = RC[h % 2]
        c = h // 2
        pb = (h % 2) * 64
        if exact_recip:
            R.op("dve", lambda e: e.reciprocal(out=rc[64:128, 0:nq], in_=po[64:128, 0:nq]), reads=[rpo], writes=[rrc])
        else:
            R.op("act", lambda e: e.activation(out=rc[64:128, 0:nq], in_=po[64:128, 0:nq], func=AF.Ln), reads=[rpo], writes=[rrc])
            R.op("act", lambda e: e.activation(out=rc[64:128, 0:nq], in_=rc[64:128, 0:nq], func=AF.Exp, scale=-1.0), reads=[rrc], writes=[rrc])
        R.op("dve", lambda e: e.tensor_tensor(out=OT[pb:pb + 64, c, qoff:qoff + nq], in0=po[0:64, 0:nq],
                                              in1=rc[64:128, 0:nq], op=ALU.mult),
             reads=[rpo, rrc], writes=[r_ot])

    def build_MT(ia):
        R.dma("sp", lambda e: e.dma_start(out=ETS[:, 64:384], in_=rel_bias[ia, :, :]), r_ets, writes=[r_ets])
        R.op("dve", lambda e: e.tensor_copy(out=ETS[:, 0:64], in_=bc(ETS[:, 64:65], 1, 64)), reads=[r_ets], writes=[r_ets])
        R.op("dve", lambda e: e.tensor_copy(out=ETS[:, 384:768], in_=bc(ETS[:, 383:384], 1, 384)), reads=[r_ets], writes=[r_ets])
        R.dma("sp", lambda e: e.dma_start(out=etab, in_=ETS[:, :]), r_ets, reads=[r_ets], writes=[r_etab])
        R.dma("sp", lambda e: e.dma_start(out=etab_rep, in_=bass.AP(etab.tensor, 0, [[768, NH], [0, 128], [1, 768]])),
              r_etabrep, reads=[r_etab], writes=[r_etabrep])
        for h in range(NH):
            src = bass.AP(etab_rep.tensor, h * 128 * 768 + 127, [[767, 128], [128, 5], [1, 128]])
            R.dma("pool", lambda e, h=h, src=src: e.dma_start(out=MT[:, h, :, :], in_=src), r_mt, reads=[r_etabrep], writes=[r_mt])
        R.op("pool", lambda e: e.memset(MT[64:128, :, 0, 0:64], NEG), writes=[r_mt])
        R.op("pool", lambda e: e.memset(MT[0:64, :, 4, 64:128], NEG), writes=[r_mt])

    def attn_A_prompt(tl, par):
        t = tl["idx"]
        for h in range(NH):
            c = h // 2
            pb = (h % 2) * 64
            kts = []
            for g in range(max(0, 4 * t - 4), 4 * t + 4):
                u_lo = max(0, g - 4 * t)
                u_hi = min(3, g - 4 * t + 4)
                src = par if g >= 4 * t else 1 - par
                ku = g % 4
                nu = u_hi - u_lo + 1
                rl = u_lo - (g - 4 * t)
                kts.append(dict(k=KT[src][0][pb:pb + 64, c, ku * 128:(ku + 1) * 128], rk=KT[src][1],
                                v=VA[src][0][:, h, ku, :], rv=VA[src][1], nk=128, c0=u_lo * 128, c1=(u_hi + 1) * 128,
                                groups=[(u_lo * 128, (u_hi + 1) * 128, None, None)],
                                mul=MT[:, h, rl:rl + nu, :], nu=nu))
            attend(h, kts, 0, 512, exact_recip=False, nbuf=4)

    def prep_cache_K(src_cache, li, b, ntile):
        for j in range(ntile):
            cks, rck = CKS[j % 2]
            R.dma("pool", lambda e, cks=cks, j=j: e.dma_start(out=cks[:, :], in_=src_cache[li, b, j * 128:(j + 1) * 128, :]),
                  rck, writes=[rck])
            for half in range(2):
                pt, rp = bankA()
                ptb = pt[:].bitcast(BF16)
                for c4 in range(4):
                    c = half * 4 + c4
                    R.op("pe", lambda e, c=c, c4=c4, ptb=ptb, cks=cks: e.transpose(
                        ptb[:, c4 * 128:(c4 + 1) * 128], cks[:, c * 128:(c + 1) * 128], IDB[:, :]),
                        reads=[rck, r_idb], writes=[rp])
                R.op("dve", lambda e, ptb=ptb, half=half, j=j: e.tensor_copy(
                    out=KC[:, half * 4:(half + 1) * 4, j * 128:(j + 1) * 128],
                    in_=ptb[:, 0:512].rearrange("p (c k) -> p c k", k=128)), reads=[rp], writes=[r_kc])

    def load_cache_V(src_cache, li, b, h, ntile):
        src = src_cache[li, b, 0:ntile * 128, h * 64:(h + 1) * 64].rearrange("(j p) e -> p j e", p=128)
        R.op("pool", lambda e: e.memset(VH[:, 0:ntile, 64:128], 1.0), writes=[r_vh])
        R.dma("pool", lambda e: e.dma_start(out=VH[:, 0:ntile, 0:64], in_=src), r_vh, writes=[r_vh])

    def attn_A_sample(l, tl, par):
        ia = l // 2
        for b in range(2):
            prep_cache_K(cak, ia, b, 4)
            for h in range(NH):
                c = h // 2
                pb = (h % 2) * 64
                load_cache_V(cav, ia, b, h, 4)
                kts = []
                for j in range(4):
                    rl = 4 - j
                    kts.append(dict(k=KC[pb:pb + 64, c, j * 128:(j + 1) * 128], rk=r_kc, v=VH[:, j, :], rv=r_vh, nk=128,
                                    c0=0, c1=32, groups=[(0, 32, None, None)], mul=MT[:, h, rl:rl + 1, 0:32], nu=1))
                kts.append(dict(k=KT[par][0][pb:pb + 64, c, b * 32:(b + 1) * 32], rk=KT[par][1],
                                v=VA[par][0][0:32, h, b, :], rv=VA[par][1], nk=32, c0=0, c1=32,
                                groups=[(0, 32, None, None)], mul=MT[0:32, h, 0:1, 0:32], nu=1))
                attend(h, kts, b * 32, 32)

    def logf_compute(l, tl):
        ib = l // 2
        nsub, rows = tl["nsub"], tl["rows"]
        for u in range(nsub):
            pt, rp = bankB()
            for kc in range(8):
                R.op("pe", lambda e, kc=kc, pt=pt, u=u: e.matmul(
                    pt[0:rows, 0:NH], lhsT=HT[:, kc, u * rows:(u + 1) * rows], rhs=WFG[:, kc, :], start=(kc == 0), stop=(kc == 7)),
                    reads=[r_ht, r_wfg], writes=[rp])
            R.op("dve", lambda e, pt=pt, u=u: e.tensor_tensor(out=ZT[0:rows, u, :], in0=pt[0:rows, 0:NH], in1=BFG[0:rows, :], op=ALU.add),
                 reads=[rp, r_bfg], writes=[r_zt])
        R.op("act", lambda e: e.activation(out=ZT[0:rows, 0:nsub, :], in_=ZT[0:rows, 0:nsub, :], func=AF.Exp, scale=-1.0),
             reads=[r_zt], writes=[r_zt])
        R.op("act", lambda e: e.activation(out=ZT[0:rows, 0:nsub, :], in_=ZT[0:rows, 0:nsub, :], func=AF.Ln, bias=ONESF[0:rows, 0:1]),
             reads=[r_zt, r_cst], writes=[r_zt])
        R.op("dve", lambda e: e.tensor_scalar(out=LF[0:rows, 0:nsub, :], in0=ZT[0:rows, 0:nsub, :], scalar1=-1.0, scalar2=None, op0=ALU.mult),
             reads=[r_zt], writes=[r_lf])
        if tl["kind"] == "p":
            dst = o_pbl[ib, tl["row0"]:tl["row0"] + 512, :].rearrange("(u p) h -> p u h", p=128)
        else:
            dst = o_sbl[ib, :, :].rearrange("(u p) h -> p u h", p=32)
        store_out(dst, LF[0:rows, 0:nsub, :], r_lf)

    def cumsum_tile(src_ap, rsrc, rows, fh_dst, cin, cout):
        pt, rp = bankB()
        R.op("pe", lambda e: e.matmul(pt[0:rows, 0:NH], lhsT=UTRI[0:rows, 0:rows], rhs=src_ap, start=True, stop=True),
             reads=[rsrc, r_cst], writes=[rp])
        R.op("pe", lambda e: e.matmul(pt[:, 32:32 + NH], lhsT=ONESF[0:rows, :], rhs=src_ap, start=True, stop=True, skip_group_check=True),
             reads=[rsrc, r_cst], writes=[rp])
        R.op("dve", lambda e: e.tensor_tensor(out=fh_dst, in0=pt[0:rows, 0:NH], in1=CARU[0:rows, cin, :], op=ALU.add),
             reads=[rp, r_caru], writes=[r_fh])
        R.op("dve", lambda e: e.tensor_tensor(out=CARU[:, cout, :], in0=pt[:, 32:32 + NH], in1=CARU[:, cin, :], op=ALU.add),
             reads=[rp, r_caru], writes=[r_caru])

    def attn_B_prompt(l, tl, par):
        t = tl["idx"]
        NJ = 4 * t + 4
        if t == 0:
            R.op("pool", lambda e: e.memset(CARU[:, 0, :], 0.0), writes=[r_caru])
            R.op("pool", lambda e: e.memset(QT[64:128, :, :], 0.0), writes=[r_qt])
            R.op("pool", lambda e: e.memset(QO[0:64, :, :], 0.0), writes=[r_qo])
            R.op("pool", lambda e: e.memset(KH[64:65, 0:4096], 1.0), writes=r_khs[0:4])
            R.op("pool", lambda e: e.memset(KH[32:33, 4096:8192], 1.0), writes=r_khs[4:8])
        else:
            R.op("dve", lambda e: e.tensor_copy(out=CARU[:, 0, :], in_=CARU[:, 4, :]), reads=[r_caru], writes=[r_caru])
        for u in range(4):
            cumsum_tile(LF[:, u, :], r_lf, 128, FH[:, 4 * t + u, :], u, u + 1)
        R.op("dve", lambda e: e.tensor_tensor(out=ZT[:, :, :], in0=FH[:, 4 * t:4 * t + 4, :], in1=bc(CARU[:, 4:5, :], 1, 4), op=ALU.subtract),
             reads=[r_fh, r_caru], writes=[r_zt])
        pt, rp = bankB()
        for u in range(4):
            R.op("pe", lambda e, u=u: e.transpose(pt[0:NH, u * 128:(u + 1) * 128], ZT[:, u, :], CST[:, 0:128]),
                 reads=[r_zt, r_cst], writes=[rp])
        R.op("dve", lambda e: e.tensor_copy(out=AQT[:, :], in_=pt[0:NH, :]), reads=[rp], writes=[r_aqt])
        for cc in range(8):
            R.dma("sp", lambda e, cc=cc: e.dma_start(out=QT[64:65, cc, :], in_=AQT[2 * cc:2 * cc + 1, :]), r_qt, reads=[r_aqt], writes=[r_qt])
            R.dma("sp", lambda e, cc=cc: e.dma_start(out=QO[32:33, cc, :], in_=AQT[2 * cc + 1:2 * cc + 2, :]), r_qo, reads=[r_aqt], writes=[r_qo])
        kt, rkt = KT[0]
        va, rva = VA[par]
        R.dma("sp", lambda e: e.dma_start(out=khist[:, :, t * 512:(t + 1) * 512].rearrange("c p k -> p c k"), in_=kt[:, :, :]),
              rkt, reads=[rkt], writes=[r_khist])
        R.dma("sp", lambda e: e.dma_start(out=vhist[:, :, 4 * t:4 * t + 4, :].rearrange("h p j e -> p h j e"),
                                          in_=va[:, :, :, :]),
              rva, reads=[rva], writes=[r_vhist])
        NBK = (NJ + 7) // 8
        for h in range(NH):
            c = h // 2
            pb = (h % 2) * 64
            orow = 64 if h % 2 == 0 else 63
            vslot = {}
            kslot = {}
            for b in range(NBK):
                j0, j1 = b * 8, min(NJ, b * 8 + 8)
                rk_ = "ke" if h % 2 == 0 else "ko"
                sk = (ring[rk_] % 4) + (0 if h % 2 == 0 else 4)
                ring[rk_] += 1
                kslot[b] = sk
                R.dma("sp", lambda e, c=c, pb=pb, sk=sk, j0=j0, j1=j1: e.dma_start(
                    out=KH[pb:pb + 64, sk * 1024:sk * 1024 + (j1 - j0) * 128], in_=khist[c, pb:pb + 64, j0 * 128:j1 * 128]),
                    r_khs[sk], reads=[r_khist], writes=[r_khs[sk]])
                sv = ring["v"] % 8
                ring["v"] += 1
                vslot[b] = sv
                R.dma("sp", lambda e, h=h, sv=sv, j0=j0, j1=j1: e.dma_start(
                    out=VH[:, sv * 8:sv * 8 + (j1 - j0), :], in_=vhist[h, :, j0:j1, :]),
                    r_vhs[sv], reads=[r_vhist], writes=[r_vhs[sv]])
            R.op("dve", lambda e, h=h: e.tensor_tensor(
                out=BI[:, 0, 0:NJ], in0=bc(CARU[:, 4, h:h + 1], 1, NJ), in1=FH[:, 0:NJ, h], op=ALU.subtract),
                reads=[r_caru, r_fh], writes=[r_bi])
            kts = []
            for j in range(NJ):
                d = j - 4 * t
                u0 = max(0, d)
                if d < 0:
                    groups = [(0, 512, BI[:, 0, j:j + 1], None)]
                else:
                    groups = [(d * 128, (d + 1) * 128, BI[:, 0, j:j + 1], MNEGD)]
                    if d < 3:
                        groups.append(((d + 1) * 128, 512, BI[:, 0, j:j + 1], None))
                sk = kslot[j // 8]
                sv = vslot[j // 8]
                kts.append(dict(k=KH[:, sk * 1024 + (j % 8) * 128:sk * 1024 + (j % 8 + 1) * 128], rk=r_khs[sk],
                                v=VH[:, sv * 8 + j % 8, :], rv=r_vhs[sv], nk=128,
                                c0=u0 * 128, c1=512, groups=groups, mul=None))
            if h % 2 == 0:
                attend(h, kts, 0, 512, qap=QT[:, c, :], rq=r_qt, nbuf=4)
            else:
                attend(h, kts, 0, 512, qap=QO[:, c, :], rq=r_qo, nbuf=4)

    def attn_B_sample(l, tl, par):
        ib = l // 2
        for b in range(2):
            prep_cache_K(cbk, ib, b, 8)
            R.dma("sp", lambda e, b=b: e.dma_start(out=LFC[:, :, :], in_=cbl[ib, b, :, :].rearrange("(j p) h -> p j h", p=128)),
                  r_lfc, writes=[r_lfc])
            R.op("pool", lambda e: e.memset(CARU[:, 0, :], 0.0), writes=[r_caru])
            for j in range(8):
                cumsum_tile(LFC[:, j, :], r_lfc, 128, FH[:, j, :], j % 2, (j + 1) % 2)
            cumsum_tile(LF[0:32, b, :], r_lf, 32, FH[0:32, 8, :], 0, 1)
            for h in range(NH):
                c = h // 2
                pb = (h % 2) * 64
                load_cache_V(cbv, ib, b, h, 8)
                R.op("dve", lambda e, h=h: e.tensor_tensor(
                    out=BI[:, 0, 0:8], in0=bc(CARU[:, 1, h:h + 1], 1, 8), in1=FH[:, 0:8, h], op=ALU.subtract),
                    reads=[r_caru, r_fh], writes=[r_bi])
                R.op("dve", lambda e, h=h: e.tensor_tensor(
                    out=BI[0:32, 0, 8:9], in0=CARU[0:32, 1, h:h + 1], in1=FH[0:32, 8, h:h + 1], op=ALU.subtract),
                    reads=[r_caru, r_fh], writes=[r_bi])
                kts = []
                for j in range(8):
                    kts.append(dict(k=KC[pb:pb + 64, c, j * 128:(j + 1) * 128], rk=r_kc, v=VH[:, j, :], rv=r_vh, nk=128,
                                    c0=0, c1=32, groups=[(0, 32, BI[:, 0, j:j + 1], None)], mul=None))
                kts.append(dict(k=KT[0][0][pb:pb + 64, c, b * 32:(b + 1) * 32], rk=KT[0][1],
                                v=VA[par][0][0:32, h, b, :], rv=VA[par][1], nk=32, c0=0, c1=32,
                                groups=[(0, 32, BI[0:32, 0, 8:9], MNEGD[0:32, 0:32])], mul=None))
                attend(h, kts, b * 32, 32)

    def residual_add(tl, u, half, pt, rp, w):
        rows = tl["rows"]
        b = 0 if tl["kind"] == "p" else u
        tmp, rt = TMP[(2 * u + half) % 2]
        R.op("dve", lambda e: e.tensor_tensor(out=tmp[0:rows, :], in0=pt[0:rows, :], in1=GA[0:rows, w, b, half * 512:(half + 1) * 512], op=ALU.mult),
             reads=[rp, r_ga], writes=[rt])
        R.op("pool", lambda e: e.tensor_tensor(out=XT[0:rows, u, half * 512:(half + 1) * 512], in0=XT[0:rows, u, half * 512:(half + 1) * 512],
                                               in1=tmp[0:rows, :], op=ALU.add), reads=[rt, r_xt], writes=[r_xt])

    def oproj(l, tl):
        nsub, rows = tl["nsub"], tl["rows"]
        for half in range(2):
            slot, rs = wload("o", l, wo_b, 0, 8, half * 512)
            for u in range(nsub):
                pt, rp = bankB()
                for kc in range(8):
                    R.op("pe", lambda e, kc=kc, pt=pt, u=u, slot=slot: e.matmul(
                        pt[0:rows, :], lhsT=OT[:, kc, u * rows:(u + 1) * rows], rhs=slot[:, kc, :], start=(kc == 0), stop=(kc == 7)),
                        reads=[rs, r_ot], writes=[rp])
                residual_add(tl, u, half, pt, rp, 0)

    def ffn(l, tl):
        T, nsub, rows = tl["T"], tl["nsub"], tl["rows"]
        for blk in range(11):
            slot, rs = wload("gu", l, wgu_b, 0, 8, blk * 512)
            for q4 in range(4):
                col = blk * 4 + q4
                pt, rp = bankA()
                for kc in range(8):
                    R.op("pe", lambda e, kc=kc, pt=pt, q4=q4, slot=slot: e.matmul(
                        pt[:, 0:T], lhsT=slot[:, kc, q4 * 128:(q4 + 1) * 128], rhs=HT[:, kc, 0:T], start=(kc == 0), stop=(kc == 7)),
                        reads=[rs, r_ht], writes=[rp])
                if col < 22:
                    R.op("act", lambda e, pt=pt, col=col: e.activation(out=AT[:, col, 0:T], in_=pt[:, 0:T], func=AF.Silu),
                         reads=[rp], writes=[r_at])
                else:
                    m = col - 22
                    R.op("dve", lambda e, pt=pt, m=m: e.tensor_tensor(out=AT[:, m, 0:T], in0=pt[:, 0:T], in1=AT[:, m, 0:T], op=ALU.mult),
                         reads=[rp, r_at], writes=[r_at])
        for half in range(2):
            accs = [PS[4 + u] for u in range(nsub)]
            for gi, (m0, nm) in enumerate(((0, 8), (8, 8), (16, 6))):
                slot, rs = wload("dn", l, wdn_b, m0 * 128, nm, half * 512)
                for u in range(nsub):
                    pt, rp = accs[u]
                    for mm in range(nm):
                        R.op("pe", lambda e, mm=mm, pt=pt, u=u, slot=slot, m0=m0, gi=gi, nm=nm: e.matmul(
                            pt[0:rows, :], lhsT=AT[:, m0 + mm, u * rows:(u + 1) * rows], rhs=slot[:, mm, :],
                            start=(gi == 0 and mm == 0), stop=(gi == 2 and mm == nm - 1)), reads=[rs, r_at], writes=[rp])
            for u in range(nsub):
                residual_add(tl, u, half, accs[u][0], accs[u][1], 1)

    def final_out(tl):
        nsub, rows = tl["nsub"], tl["rows"]
        rstd_calc(tl)
        R.dma("sp", lambda e: e.dma_start(out=GOUT[:], in_=bass.AP(g_out.tensor, 0, [[0, 128], [1, D]])), r_gout, writes=[r_gout])
        for u in range(nsub):
            R.op("dve", lambda e, u=u: e.scalar_tensor_tensor(
                out=XT[0:rows, u, :], in0=XT[0:rows, u, :], scalar=SS[0:rows, 4 + u:5 + u], in1=GOUT[0:rows, :],
                op0=ALU.mult, op1=ALU.mult), reads=[r_xt, r_ss, r_gout], writes=[r_xt])
        if tl["kind"] == "p":
            dst = y_p[tl["row0"]:tl["row0"] + 512, :].rearrange("(u p) d -> p u d", p=128)
        else:
            dst = y_s.rearrange("(u p) d -> p u d", p=32)
        store_out(dst, XT[0:rows, 0:nsub, :], r_xt)

    try:
        for l in range(NL):
            isB = (l % 2 == 1)
            ck("pre-mod")
            modulation(l)
            ck("mod")
            if not isB:
                build_MT(l // 2)
                ck("MT")
            else:
                ib = l // 2
                R.dma("pool", lambda e, ib=ib: e.dma_start(out=WFG[:, :, :], in_=w_fg[ib, :, :].rearrange("(kc p) h -> p kc h", p=128)),
                      r_wfg, writes=[r_wfg])
                R.dma("sp", lambda e, ib=ib: e.dma_start(out=BFG[:, :], in_=bass.AP(b_fg.tensor, ib * NH, [[0, 128], [1, NH]])),
                      r_bfg, writes=[r_bfg])
            for tl in tiles:
                nsub, rows = tl["nsub"], tl["rows"]
                ti = tl["idx"]
                par = ti % 2
                if tl["kind"] == "s":
                    csbc_fill()
                    mod_ga(l, [(1, 0), (2, 1)])
                if l == 0:
                    src = xp[tl["row0"]:tl["row0"] + 512, :] if tl["kind"] == "p" else xs
                    rsrc = []
                else:
                    src = xscr[tl["row0"]:tl["row0"] + tl["T"], :]
                    rsrc = [r_xscr[ti]]
                R.dma("sp", lambda e, src=src, rows=rows, nsub=nsub: e.dma_start(
                    out=XT[0:rows, 0:nsub, :], in_=src.rearrange("(u p) d -> p u d", p=rows)), r_xt, reads=rsrc, writes=[r_xt])
                norm_to_hT(tl, 0)
                ck("norm1 l%d t%d" % (l, ti))
                need_ktok, kv_dst = qkv(l, tl, par)
                ck("qkv l%d t%d" % (l, ti))
                if isB:
                    logf_compute(l, tl)
                    ck("logf l%d t%d" % (l, ti))
                if tl["kind"] == "p":
                    if isB:
                        attn_B_prompt(l, tl, par)
                    else:
                        attn_A_prompt(tl, par)
                else:
                    if isB:
                        attn_B_sample(l, tl, par)
                    else:
                        attn_A_sample(l, tl, par)
                ck("attn l%d t%d" % (l, ti))
                oproj(l, tl)
                ck("oproj l%d t%d" % (l, ti))
                norm_to_hT(tl, 1)
                ffn(l, tl)
                ck("ffn l%d t%d" % (l, ti))
                if l == NL - 1:
                    final_out(tl)
                else:
                    R.dma("sp", lambda e, tl=tl, rows=rows, nsub=nsub: e.dma_start(
                        out=xscr[tl["row0"]:tl["row0"] + tl["T"], :].rearrange("(u p) d -> p u d", p=rows), in_=XT[0:rows, 0:nsub, :]),
                        r_xt, reads=[r_xt], writes=[r_xscr[ti]])
                ck("end l%d t%d" % (l, ti))
    except _Stop:
        pass

    R.wait_all("sp", [ro.last_w for ro in out_res] + [rr.last_w for rr in (r_xt, r_ht, r_at, r_ga, r_mt, r_gs) if rr.last_w is not None])
    print("instr counts", {e: len(R.ins[e]) for e in ENGS}, "dma sems", len(R.dma_sems), flush=True)
    R.replay()
    st.close()
    return nc


def make_consts():
    c = np.zeros((128, 640), np.float32)
    c[:, 0:128] = np.eye(128, dtype=np.float32)
    k = np.arange(128)[:, None]
    m = np.arange(128)[None, :]
    c[:, 128:256] = (k <= m).astype(np.float32)
    c[:, 256:384] = 1.0
    c[:, 384:512] = np.where(m >= k, 0.0, NEG).astype(np.float32)
    c[:, 512] = 1e-6
    return c


_CACHE = {}


def run(inputs, NPT=16, NL=DEPTH, ncores=NCORES):
    key = (NPT, NL)
    if key not in _CACHE:
        _CACHE[key] = build(NPT, NL)
    nc = _CACHE[key]
    f = lambda a: np.ascontiguousarray(np.asarray(a), dtype=np.float32)
    I = {k: f(v) for k, v in inputs.items()}
    cstv = make_consts()
    in_maps = []
    for i in range(ncores):
        m = {
            "xp": I["x_prompt"][i], "xs": I["x_sample"][2 * i:2 * i + 2].reshape(64, D),
            "cak": np.ascontiguousarray(I["cache_a_k"][:, 2 * i:2 * i + 2].reshape(2, 2, APAST, D)),
            "cav": np.ascontiguousarray(I["cache_a_v"][:, 2 * i:2 * i + 2].reshape(2, 2, APAST, D)),
            "cbk": np.ascontiguousarray(I["cache_b_k"][:, 2 * i:2 * i + 2].reshape(2, 2, PAST, D)),
            "cbv": np.ascontiguousarray(I["cache_b_v"][:, 2 * i:2 * i + 2].reshape(2, 2, PAST, D)),
            "cbl": np.ascontiguousarray(I["cache_b_logf"][:, 2 * i:2 * i + 2]),
            "c3": np.ascontiguousarray(np.concatenate([I["c_prompt"][i:i + 1], I["c_sample"][2 * i:2 * i + 2]], axis=0)),
            "w_mod": I["w_mod"], "b_mod": I["b_mod"], "g_mix": I["g_mix"], "g_ffn": I["g_ffn"],
            "w_qkv": I["w_qkv"], "w_o": I["w_o"], "rel_bias": I["rel_bias"], "w_fgate": I["w_fgate"],
            "b_fgate": I["b_fgate"], "w_gu": I["w_gu"], "w_down": I["w_down"], "g_out": I["g_out"].reshape(1, D),
            "cst": cstv,
        }
        in_maps.append(m)
    res = run_bass_kernel_spmd(nc, in_maps, core_ids=list(range(ncores)))
    return res.results


def kernel(**inputs):
    rs = run(inputs)
    y_p = np.stack([r["y_p"] for r in rs], 0)
    y_s = np.concatenate([r["y_s"].reshape(2, 32, D) for r in rs], 0)

    def pst(name, rows):
        return np.stack([r[name] for r in rs], 1).reshape(2, NCORES, rows, NH, HD)

    def sst(name):
        return np.concatenate([r[name].reshape(2, 2, 32, NH, HD) for r in rs], 1)

    pbl = np.stack([r["pbl"] for r in rs], 1)
    sbl = np.concatenate([r["sbl"].reshape(2, 2, 32, NH) for r in rs], 1)
    outs = (y_p, y_s, pst("pak", APAST), pst("pav", APAST), pst("pbk", SEQ), pst("pbv", SEQ), pbl,
            sst("sak"), sst("sav"), sst("sbk"), sst("sbv"), sbl)
    return tuple(np.ascontiguousarray(o, dtype=np.float32) for o in outs)
```
